# Optimizing a Trainium2 kernel written in Bass

```python
import jax, jax.numpy as jnp
from jax import lax
import numpy as np

D_MODEL = 1024
BATCH = 16
SEQ = 2048
DEPTH = 1

HEAD_DIM = 64
N_HEADS = D_MODEL // HEAD_DIM
N_MOBA = N_HEADS // 2
N_FOX = N_HEADS - N_MOBA
W_MOBA = N_MOBA * HEAD_DIM
W_FOX = N_FOX * HEAD_DIM
MOBA_BLOCK = 256
MOBA_TOPK = 3
MOBA_QCHUNK = 16
FOX_QBLOCK = 128
N_BUCKETS = 32
MAX_DISTANCE = 128
D_FF = 4 * D_MODEL
PLE_DIM = 256
EPS = 1e-6
NEG = -1e30
SCALE = HEAD_DIM ** -0.5
IN_WIDTH = 3 * W_MOBA + 3 * W_FOX + N_FOX

kernel_name = "hymba_moba_fox_ple_layer"


def rmsnorm(x, g):
    xf = x.astype(jnp.float32)
    y = xf * lax.rsqrt(jnp.mean(xf * xf, axis=-1, keepdims=True) + EPS)
    return (y * g.astype(jnp.float32)).astype(x.dtype)


def t5_bucket(rel):
    rel = jnp.maximum(rel, 0)
    max_exact = N_BUCKETS // 2
    relf = jnp.maximum(rel, max_exact).astype(jnp.float32)
    large = max_exact + (jnp.log(relf / max_exact) / np.log(MAX_DISTANCE / max_exact)
                         * (N_BUCKETS - max_exact)).astype(jnp.int32)
    large = jnp.minimum(large, N_BUCKETS - 1)
    return jnp.where(rel < max_exact, rel, large)


def moba_attention(q, k, v, rel_bias):
    B, H, S, Dh = q.shape
    nb = -(-S // MOBA_BLOCK)
    pad = nb * MOBA_BLOCK - S
    k_sel = min(MOBA_TOPK, nb)
    kb = jnp.pad(k, ((0, 0), (0, 0), (0, pad), (0, 0))).reshape(B, H, nb, MOBA_BLOCK, Dh)
    vb = jnp.pad(v, ((0, 0), (0, 0), (0, pad), (0, 0))).reshape(B, H, nb, MOBA_BLOCK, Dh)
    kmean = jnp.mean(kb.astype(jnp.float32), axis=3)
    relT = rel_bias.T
    bi = jnp.arange(B)[:, None, None, None]
    hi = jnp.arange(H)[None, :, None, None]
    offs = jnp.arange(MOBA_BLOCK)
    blk_ids = jnp.arange(nb)
    n_sel = k_sel * MOBA_BLOCK

    def chunk(c):
        t0 = c * MOBA_QCHUNK
        qc = lax.dynamic_slice_in_dim(q, t0, MOBA_QCHUNK, axis=2)
        tq = t0 + jnp.arange(MOBA_QCHUNK)
        own = t0 // MOBA_BLOCK
        gate = jnp.einsum('bhqd,bhnd->bhqn', qc.astype(jnp.float32), kmean)
        gate = jnp.where(blk_ids < own, gate, NEG)
        _, idx = lax.top_k(gate, k_sel)
        valid = idx < own
        ksel = kb[bi, hi, idx]
        vsel = vb[bi, hi, idx]
        s_sel = jnp.einsum('bhqd,bhqkld->bhqkl', qc, ksel).astype(jnp.float32) * SCALE
        pos_sel = idx[..., None] * MOBA_BLOCK + offs
        bucket_sel = t5_bucket(tq[None, None, :, None, None] - pos_sel)
        s_sel = s_sel + relT[hi[..., None], bucket_sel].astype(jnp.float32)
        s_sel = jnp.where(valid[..., None], s_sel, NEG).reshape(B, H, MOBA_QCHUNK, n_sel)
        kown = lax.dynamic_index_in_dim(kb, own, axis=2, keepdims=False)
        vown = lax.dynamic_index_in_dim(vb, own, axis=2, keepdims=False)
        rel_own = tq[:, None] - (own * MOBA_BLOCK + offs)[None, :]
        s_own = jnp.einsum('bhqd,bhld->bhql', qc, kown).astype(jnp.float32) * SCALE
        s_own = s_own + relT[:, t5_bucket(rel_own)].astype(jnp.float32)[None]
        s_own = jnp.where(rel_own >= 0, s_own, NEG)
        probs = jax.nn.softmax(jnp.concatenate([s_sel, s_own], axis=-1), axis=-1)
        p_sel = probs[..., :n_sel].reshape(B, H, MOBA_QCHUNK, k_sel, MOBA_BLOCK)
        p_own = probs[..., n_sel:]
        out = (jnp.einsum('bhqkl,bhqkld->bhqd', p_sel.astype(v.dtype), vsel)
               + jnp.einsum('bhql,bhld->bhqd', p_own.astype(v.dtype), vown))
        return out

    out = lax.map(chunk, jnp.arange(S // MOBA_QCHUNK))
    return jnp.moveaxis(out, 0, 2).reshape(B, H, S, Dh)


def forgetting_attention(q, k, v, log_f):
    B, H, S, Dh = q.shape
    cum = jnp.cumsum(log_f, axis=-1)
    pos = jnp.arange(S)

    def block(i):
        t0 = i * FOX_QBLOCK
        qb = lax.dynamic_slice_in_dim(q, t0, FOX_QBLOCK, axis=2)
        cq = lax.dynamic_slice_in_dim(cum, t0, FOX_QBLOCK, axis=2)
        tq = t0 + jnp.arange(FOX_QBLOCK)
        s = jnp.einsum('bhqd,bhkd->bhqk', qb, k).astype(jnp.float32) * SCALE
        s = s + cq[..., None] - cum[:, :, None, :]
        s = jnp.where(tq[:, None] >= pos[None, :], s, NEG)
        probs = jax.nn.softmax(s, axis=-1)
        return jnp.einsum('bhqk,bhkd->bhqd', probs.astype(v.dtype), v)

    out = lax.map(block, jnp.arange(S // FOX_QBLOCK))
    return jnp.moveaxis(out, 0, 2).reshape(B, H, S, Dh)


def setup_inputs(seed: int = 0) -> dict:
    key = jax.random.key(seed)
    ks = jax.random.split(key, 20)
    f32 = jnp.float32
    nrm = lambda k, shape, s: jax.random.normal(k, shape, f32) * s
    gain = lambda k, shape: 1.0 + 0.05 * jax.random.normal(k, shape, f32)
    return {
        "x": nrm(ks[0], (BATCH, SEQ, D_MODEL), 1.0),
        "p": nrm(ks[1], (DEPTH, BATCH, SEQ, PLE_DIM), 1.0),
        "rel_bias": nrm(ks[2], (N_BUCKETS, N_MOBA), 0.5),
        "g_attn": gain(ks[3], (DEPTH, D_MODEL)),
        "w_in": nrm(ks[4], (DEPTH, D_MODEL, IN_WIDTH), D_MODEL ** -0.5),
        "b_f": 3.0 + 0.1 * jax.random.normal(ks[5], (DEPTH, N_FOX), f32),
        "gq_moba": gain(ks[6], (DEPTH, HEAD_DIM)),
        "gk_moba": gain(ks[7], (DEPTH, HEAD_DIM)),
        "gq_fox": gain(ks[8], (DEPTH, HEAD_DIM)),
        "gk_fox": gain(ks[9], (DEPTH, HEAD_DIM)),
        "w_out": nrm(ks[10], (DEPTH, D_MODEL, D_MODEL), D_MODEL ** -0.5),
        "g_mlp": gain(ks[11], (DEPTH, D_MODEL)),
        "w_up": nrm(ks[12], (DEPTH, D_MODEL, D_FF), D_MODEL ** -0.5),
        "w_down": nrm(ks[13], (DEPTH, D_FF, D_MODEL), D_FF ** -0.5),
        "g_ple": gain(ks[14], (DEPTH, D_MODEL)),
        "w_ple_gate": nrm(ks[15], (DEPTH, D_MODEL, D_MODEL), D_MODEL ** -0.5),
        "w_ple_proj": nrm(ks[16], (DEPTH, PLE_DIM, D_MODEL), PLE_DIM ** -0.5),
    }


def reference(x, p, rel_bias, g_attn, w_in, b_f, gq_moba, gk_moba, gq_fox, gk_fox,
              w_out, g_mlp, w_up, w_down, g_ple, w_ple_gate, w_ple_proj):
    B, S, D = x.shape

    def heads(t, n):
        return t.reshape(B, S, n, HEAD_DIM).transpose(0, 2, 1, 3)

    for i in range(DEPTH):
        h = rmsnorm(x, g_attn[i])
        proj = h @ w_in[i]
        c0 = 0
        qm = proj[..., c0:c0 + W_MOBA]; c0 += W_MOBA
        km = proj[..., c0:c0 + W_MOBA]; c0 += W_MOBA
        vm = proj[..., c0:c0 + W_MOBA]; c0 += W_MOBA
        qf = proj[..., c0:c0 + W_FOX]; c0 += W_FOX
        kf = proj[..., c0:c0 + W_FOX]; c0 += W_FOX
        vf = proj[..., c0:c0 + W_FOX]; c0 += W_FOX
        f_logit = proj[..., c0:c0 + N_FOX]

        qm = rmsnorm(heads(qm, N_MOBA), gq_moba[i])
        km = rmsnorm(heads(km, N_MOBA), gk_moba[i])
        vm = heads(vm, N_MOBA)
        qf = rmsnorm(heads(qf, N_FOX), gq_fox[i])
        kf = rmsnorm(heads(kf, N_FOX), gk_fox[i])
        vf = heads(vf, N_FOX)
        log_f = jax.nn.log_sigmoid((f_logit + b_f[i]).astype(jnp.float32)).transpose(0, 2, 1)

        o_moba = moba_attention(qm, km, vm, rel_bias)
        o_fox = forgetting_attention(qf, kf, vf, log_f)
        o = jnp.concatenate([o_moba, o_fox], axis=1).transpose(0, 2, 1, 3).reshape(B, S, D)
        x = x + o @ w_out[i]

        h = rmsnorm(x, g_mlp[i])
        x = x + jnp.square(jax.nn.relu(h @ w_up[i])) @ w_down[i]

        gate = jax.nn.sigmoid(rmsnorm(x, g_ple[i]) @ w_ple_gate[i])
        x = x + gate * (p[i] @ w_ple_proj[i])
    return x
```

```python
import contextlib
import numpy as np
import ml_dtypes
import concourse.bass as bass
import concourse.mybir as mybir
from concourse.bass_utils import run_bass_kernel_spmd

F32 = mybir.dt.float32
BF16 = mybir.dt.bfloat16
AF = mybir.ActivationFunctionType
ALU = mybir.AluOpType
AX = mybir.AxisListType

S = 2048
D = 1024
NT = S // 128
EPS = 1e-6
SCALE = 0.125
NCORES = 8
NBATCH = 16
NB = NBATCH // NCORES
NSLAB = 3
OT_ALIAS = {0: ["xt0"], 1: ["xt1"], 2: ["hb0", "hb1"], 3: ["a1junk"]}
IPIPE = 4


class Op:
    __slots__ = ("eng", "fn", "reads", "writes", "idx", "deps", "sig", "signum",
                 "waits", "lane", "ndma", "dmacum", "know", "name")


class Prog:
    STREAMS = ("pe", "act", "dve", "pool", "sp")

    def __init__(self, nc):
        self.nc = nc
        self.ops = []
        self.last_w = {}
        self.readers = {}
        self.lane_cnt = {}
        self.lane_lastop = {}
        self.last_on = {}

    @staticmethod
    def _stream(eng):
        return {"q_sp": "sp", "q_act": "act", "q_pool": "pool"}.get(eng, eng)

    def add(self, eng, fn, reads=(), writes=(), lane=None, ndma=1, name="", after=()):
        op = Op()
        op.eng, op.fn, op.reads, op.writes = eng, fn, tuple(reads), tuple(writes)
        op.idx = len(self.ops)
        op.lane, op.ndma, op.name = lane, ndma, name
        op.sig, op.signum, op.waits, op.dmacum, op.know = False, 0, {}, 0, None
        deps = {}
        for r in op.reads:
            lw = self.last_w.get(r)
            if lw is not None:
                deps[lw.idx] = (lw, "raw")
        for w in op.writes:
            lw = self.last_w.get(w)
            if lw is not None and lw.idx not in deps:
                deps[lw.idx] = (lw, "waw")
            for rd in self.readers.get(w, ()):
                if rd.idx not in deps:
                    deps[rd.idx] = (rd, "war")
        for a in after:
            deps[a.idx] = (a, "raw")
        out = []
        seng = self._stream(eng)
        for d, kind in deps.values():
            if d is op:
                continue
            if d.lane is None and self._stream(d.eng) == seng:
                if seng == "pe":
                    continue
            out.append(d)
        op.deps = out
        for r in op.reads:
            self.readers.setdefault(r, []).append(op)
        for w in op.writes:
            self.last_w[w] = op
            self.readers[w] = []
        if lane is not None:
            c = self.lane_cnt.get(lane, 0) + 16 * ndma
            self.lane_cnt[lane] = c
            op.dmacum = c
            self.lane_lastop[lane] = op
        else:
            self.last_on[seng] = op
        self.ops.append(op)
        return op

    def barrier(self):
        deps = [o for o in self.last_on.values()] + list(self.lane_lastop.values())
        b = self.add("sp", lambda e: e.nop(), after=deps, name="barrier")
        for s in ("pe", "act", "dve", "pool"):
            self.add(s, lambda e: e.nop(), after=[b], name="barrier_w")
        self.last_w.clear()
        self.readers.clear()

    def finalize(self, stack):
        nc = self.nc
        for op in self.ops:
            for d in op.deps:
                d.sig = True
        cnt = {}
        for op in self.ops:
            if op.lane is None and op.sig:
                s = self._stream(op.eng)
                cnt[s] = cnt.get(s, 0) + 1
                op.signum = cnt[s]
        self.sems = {}
        for s in self.STREAMS:
            self.sems[s] = stack.enter_context(nc.semaphore("sem_" + s))
        for ln in self.lane_cnt:
            self.sems[("lane", ln)] = stack.enter_context(nc.semaphore("dl_%s" % (ln,)))
        know = {s: {} for s in self.STREAMS}
        lane_prev = {}
        nw = 0
        for op in self.ops:
            s = self._stream(op.eng)
            K = know[s]
            need = {}
            dk = []
            for d in op.deps:
                if d.lane is not None:
                    key, val = ("lane", d.lane), d.dmacum
                else:
                    key, val = self._stream(d.eng), d.signum
                dk.append((d, key, val))
                if K.get(key, 0) < val and need.get(key, 0) < val:
                    need[key] = val
            for d, key, val in dk:
                if key in need:
                    for k2, v2 in d.know.items():
                        if K.get(k2, 0) < v2:
                            K[k2] = v2
            for key, val in need.items():
                if K.get(key, 0) < val:
                    K[key] = val
            op.waits = need
            nw += len(need)
            kk = dict(K)
            if op.lane is not None:
                prev = lane_prev.get(op.lane, 0)
                assert K.get(("lane", op.lane), 0) >= prev, "DMA lane %s reuse hazard at %s" % (op.lane, op.name)
                lane_prev[op.lane] = op.dmacum
                kk[("lane", op.lane)] = op.dmacum
            elif op.sig:
                kk[s] = max(kk.get(s, 0), op.signum)
            op.know = kk
        self.nwaits = nw
        return cnt

    def emit(self, block, final_waits=()):
        streams = {s: [] for s in self.STREAMS}
        for op in self.ops:
            streams[self._stream(op.eng)].append(op)
        sems = self.sems

        def run(e, lst, extra=None):
            for op in lst:
                for key, val in op.waits.items():
                    e.wait_ge(sems[key], val)
                r = op.fn(e)
                if op.lane is not None:
                    if not isinstance(r, (list, tuple)):
                        r = [r]
                    assert len(r) == op.ndma, (op.name, len(r), op.ndma)
                    for ins in r:
                        ins.then_inc(sems[("lane", op.lane)], 16)
                elif op.sig:
                    r.then_inc(sems[self._stream(op.eng)], 1)
            if extra:
                for key, val in extra:
                    e.wait_ge(sems[key], val)

        @block.tensor
        def _(e):
            run(e, streams["pe"])

        @block.scalar
        def _(e):
            run(e, streams["act"])

        @block.vector
        def _(e):
            run(e, streams["dve"])

        @block.gpsimd
        def _(e):
            run(e, streams["pool"])

        @block.sync
        def _(e):
            run(e, streams["sp"], extra=final_waits)


def _t5_bucket_np(rel):
    rel = np.maximum(rel, 0)
    relf = np.maximum(rel, 16).astype(np.float32)
    v = (np.log(relf / np.float32(16.0)).astype(np.float32) / np.float32(np.log(128 / 16))).astype(np.float32) * np.float32(16.0)
    large = 16 + v.astype(np.int32)
    large = np.minimum(large, 31)
    return np.where(rel < 16, rel, large)


def _consts():
    bf = ml_dtypes.bfloat16
    c = {}
    c["c_ident"] = np.eye(128, dtype=np.float32).astype(bf)
    c["c_antij"] = np.eye(128, dtype=np.float32)[::-1].copy().astype(bf)
    u = np.arange(128)
    c["c_cmx"] = np.where(u[:, None] + u[None, :] <= 127, 0.0, -65536.0).astype(np.float32).astype(bf)
    se = np.zeros((128, 128), np.float32); se[64, :] = 1.0
    so = np.zeros((128, 128), np.float32); so[0, :] = 1.0
    c["c_sele"] = se.astype(bf)
    c["c_selo"] = so.astype(bf)
    c["c_utri"] = (u[:, None] <= u[None, :]).astype(np.float32)
    c["c_onesf"] = np.ones((128, 128), np.float32)
    oh = np.zeros((33, 384), np.float32)
    for k in range(384):
        dist = 255 - k
        if dist >= 0:
            oh[int(_t5_bucket_np(np.array([dist]))[0]), k] += 1.0
            oh[31, k] -= 1.0
        else:
            oh[32, k] = 1.0
    c["c_ohp"] = oh
    ka = np.zeros((2, 8, S), np.float32)
    for n in range(8):
        ka[0, n, n * 256:(n + 1) * 256] = -65536.0
    ka[1, 0, :] = 1.0
    c["c_kaug"] = ka.astype(bf)
    return c


class _Stop(Exception):
    pass


def build_program(nb=NB, stop=None, dbg=None):
    nc = bass.Bass("TRN2", target_bir_lowering=False)

    def din(name, shape, dt=F32):
        return nc.dram_tensor(name, list(shape), dt, kind="ExternalInput").ap()

    x_d = din("x", [nb, S, D])
    p_d = din("p", [nb, S, 256])
    rel_d = din("rel_bias", [32, 8])
    gattn_d = din("g_attn", [1, D])
    win_d = din("w_in", [D, 3080])
    bf_d = din("b_f", [1, 8])
    gqm_d = din("gq_moba", [1, 64])
    gkm_d = din("gk_moba", [1, 64])
    gqf_d = din("gq_fox", [1, 64])
    gkf_d = din("gk_fox", [1, 64])
    wout_d = din("w_out", [D, D])
    gmlp_d = din("g_mlp", [1, D])
    wup_d = din("w_up", [D, 4096])
    wdn_d = din("w_down", [4096, D])
    gple_d = din("g_ple", [1, D])
    wgate_d = din("w_ple_gate", [D, D])
    wproj_d = din("w_ple_proj", [256, D])
    ident_d = din("c_ident", [128, 128], BF16)
    antij_d = din("c_antij", [128, 128], BF16)
    cmx_d = din("c_cmx", [128, 128], BF16)
    sele_d = din("c_sele", [128, 128], BF16)
    selo_d = din("c_selo", [128, 128], BF16)
    utri_d = din("c_utri", [128, 128])
    onesf_d = din("c_onesf", [128, 128])
    ohp_d = din("c_ohp", [33, 384])
    kaug_d = din("c_kaug", [2, 8, S], BF16)
    y_d = nc.dram_tensor("y", [nb, S, D], F32, kind="ExternalOutput").ap()
    dbg_d = nc.dram_tensor("dbg", [128, 16384], BF16, kind="ExternalOutput").ap() if stop else None
    wd_d = nc.dram_tensor("wd_scr", [8, 384], BF16, kind="Internal").ap()

    win_v = win_d.rearrange("(k p) n -> p k n", p=128)
    wout_v = wout_d.rearrange("(k p) n -> p k n", p=128)
    wup_v = wup_d.rearrange("(k p) n -> p k n", p=128)
    wdn_v = wdn_d.rearrange("(j p) n -> p j n", p=128)
    wgate_v = wgate_d.rearrange("(k p) n -> p k n", p=128)
    wproj_v = wproj_d.rearrange("(k p) n -> p k n", p=128)

    st = contextlib.ExitStack()
    with st:
        def sb(name, shape, dt):
            return st.enter_context(nc.sbuf_tensor(name, list(shape), dt))

        def ps(name, shape, dt):
            return st.enter_context(nc.psum_tensor(name, list(shape), dt))

        ident = sb("ident", [128, 128], BF16)
        antij = sb("antij", [128, 128], BF16)
        cmx = sb("cmx", [128, 128], BF16)
        sele = sb("sele", [128, 128], BF16)
        selo = sb("selo", [128, 128], BF16)
        utri = sb("utri", [128, 128], F32)
        onesf = sb("onesf", [128, 128], F32)
        ohp = sb("ohp", [33, 384], F32)
        relx = sb("relx", [33, 8], F32)
        wrow = sb("wrow", [8, 384], BF16)
        XT = sb("XT", [128, 8, 2, 128], BF16)
        c31b = sb("c31b", [128, 8], F32)
        biasM = sb("biasM", [128, 8], F32)
        gq4m = sb("gq4m", [128, 4, 64], F32)
        gq4f = sb("gq4f", [128, 4, 64], F32)
        gmx = sb("gmx", [128, 4], F32)
        negMm = sb("negMm", [128, 1], F32)
        negMf = sb("negMf", [128, 1], F32)
        bfb = sb("bfb", [128, 8], F32)
        gbuf = [sb("gbuf%d" % i, [128, D], F32) for i in range(2)]
        ssx = sb("ssx", [128, 1], F32)
        rsx = sb("rsx", [128, 1], F32)
        ssq4 = sb("ssq4", [128, 2, 4], F32)
        rs4 = sb("rs4", [128, 2, 4], F32)
        z8 = sb("z8", [128, 2, 8], F32)
        l8 = sb("l8", [128, 2, 8], F32)
        pref8 = sb("pref8", [128, 8], F32)
        ncb = sb("ncb", [128, NT, 8], F32)
        cqtok = sb("cqtok", [128, 128], BF16)
        msT = sb("msT", [16, S], BF16)
        cqT = msT
        mval = sb("mval", [128, 256], BF16)
        mval2 = [mval[:, 0:128], mval[:, 128:256]]
        gsb_t = sb("gsb", [128, 2, 2, 8], F32)
        gsb2 = [gsb_t[:, 0, :, :], gsb_t[:, 1, :, :]]
        top8_t = sb("top8", [128, 2, 2, 8], F32)
        top8b = [top8_t[:, 0, :, :], top8_t[:, 1, :, :]]
        km32 = sb("km32", [64, 2, 8], F32)
        kmT = sb("kmT", [128, 2, 8], BF16)
        Re = sb("Re", [128, 512], BF16)
        Ro = sb("Ro", [128, 512], BF16)
        bcs = sb("bcs", [128, 512], F32)
        den32 = sb("den32", [128, 512], F32)
        negones = sb("negones", [128, 1], F32)
        Pt = [sb("Pt%d" % i, [128, 512], BF16) for i in range(3)]
        hT = sb("hT", [128, 8 * S], BF16)
        OT = sb("OT", [128, 8 * S], BF16)
        slabs = [sb("slab%d" % i, [128, 8192], BF16) for i in range(NSLAB)]
        ar = sb("arena", [128, 25600], BF16)
        hTv = hT[:, :].rearrange("p (k s) -> p k s", k=8)
        uTv = hT[:, :].rearrange("p (j s) -> p j s", j=32)
        OTv = OT[:, :].rearrange("p (k s) -> p k s", k=8)
        xt = [OT[:, i * 2048:(i + 1) * 2048].bitcast(F32) for i in range(2)]
        hb = [OT[:, 4096 + i * 1024:4096 + (i + 1) * 1024] for i in range(2)]
        a1junk = OT[:, 6144:8192].bitcast(F32)
        QAs = [ar[:, o:o + 4096].rearrange("p (h s) -> p h s", h=2) for o in (0, 11264)]
        KAs = [ar[:, o + 4096:o + 8192].rearrange("p (h s) -> p h s", h=2) for o in (0, 11264)]
        VAs = [ar[:, o + 8192:o + 11264].rearrange("p (t c) -> p t c", c=192) for o in (0, 11264)]
        sqs2 = [ar[:, 22528 + i * 1536:23040 + i * 1536].bitcast(F32) for i in range(2)]
        tq2 = [ar[:, 23040 + i * 1536:23552 + i * 1536].bitcast(F32) for i in range(2)]
        qkn2 = [ar[:, 23552 + i * 1536:24064 + i * 1536].rearrange("p (a c) -> p a c", c=128) for i in range(2)]
        xc = [ar[:, i * 2048:(i + 1) * 2048].bitcast(F32) for i in range(4)]
        h2T = ar[:, 8192:12288].rearrange("p (k s) -> p k s", k=8)
        pT = ar[:, 12288:13312].rearrange("p (k s) -> p k s", k=2)
        r32 = [ar[:, 13312 + i * 1024:13312 + (i + 1) * 1024].bitcast(F32) for i in range(2)]
        sig = ar[:, 15360:17408].bitcast(F32)
        tmpf = ar[:, 17408:19456].bitcast(F32)
        hbC = Pt[0][:, :]
        hbC_t = sb("hbC", [128, 2 * D], BF16)
        hbC = hbC_t[:, 0:D]
        hbC2 = [hbC_t[:, 0:D], hbC_t[:, D:2 * D]]
        ssx2 = sb("ssx2", [128, 2], F32)
        rsx2 = sb("rsx2", [128, 2], F32)
        pf2_t = sb("pf2", [128, 512], F32)
        pb2_t = sb("pb2", [128, 512], BF16)
        pf2 = [pf2_t[:, 0:256], pf2_t[:, 256:512]]
        pb2 = [pb2_t[:, 0:256], pb2_t[:, 256:512]]
        pp01 = ps("pp01", [128, 1024], F32)
        tp = ps("tp", [128, 1024], BF16)
        Sps = [ps("S%d" % i, [128, 512], F32) for i in range(3)]
        pp67 = ps("pp67", [128, 1024], F32)
        ipp = [pp01[:, 0:512], pp01[:, 512:1024]]
        Ops = [pp67[:, 0:512], pp67[:, 512:1024]]
        BCv = tp[:, :].bitcast(F32)
        Gp2 = [BCv[:, 448:464], BCv[:, 464:480]]
        CUp = BCv[:, 480:488]
        tp8 = tp[:, :].rearrange("p (k s) -> p k s", k=8)

        P = Prog(nc)
        add = P.add
        lane_n = [0]

        def dma(q, out, in_, reads=(), writes=(), lane=None, name=""):
            if lane is None:
                lane = "u%d" % lane_n[0]
                lane_n[0] += 1
            return add(q, lambda e: [e.dma_start(out=out, in_=in_)], reads=reads, writes=writes, lane=lane, name=name)

        dma("q_sp", ident[:, :], ident_d, writes=["ident"])
        dma("q_sp", antij[:, :], antij_d, writes=["antij"])
        dma("q_sp", cmx[:, :], cmx_d, writes=["cmx"])
        dma("q_sp", sele[:, :], sele_d, writes=["sele"])
        dma("q_sp", selo[:, :], selo_d, writes=["selo"])
        dma("q_sp", utri[:, :], utri_d, writes=["utri"])
        dma("q_sp", onesf[:, :], onesf_d, writes=["onesf"])
        dma("q_sp", ohp[:, :], ohp_d, writes=["ohp"])
        dma("q_sp", relx[0:32, :], rel_d, writes=["relx"])
        add("pool", lambda e: e.memset(relx[32:33, :], -8192.0), writes=["relx32"])
        dma("q_sp", c31b[:, :], rel_d[31:32, :].partition_broadcast(128), writes=["c31b"])
        dma("q_sp", bfb[:, :], bf_d.partition_broadcast(128), writes=["bfb"])
        for i, (gt, srcs) in enumerate(((gq4m, (gqm_d, gqm_d, gkm_d, gkm_d)), (gq4f, (gqf_d, gqf_d, gkf_d, gkf_d)))):
            for j, sd in enumerate(srcs):
                dma("q_sp", gt[:, j, :], sd.partition_broadcast(128), writes=["g4_%d_%d" % (i, j)])
        add("pe", lambda e: e.matmul(Ops[0][0:8, 0:384], lhsT=relx[0:33, 0:8], rhs=ohp[0:33, 0:384], start=True, stop=True),
            reads=["relx", "relx32", "ohp"], writes=["O0"])
        add("act", lambda e: e.activation(out=wrow[:, :], in_=Ops[0][0:8, 0:384], func=AF.Copy, scale=8.0),
            reads=["O0"], writes=["wrow"])
        dma("q_act", wd_d, wrow[:, :], reads=["wrow"], writes=["wd"])
        for h in range(8):
            src = bass.AP(tensor=wd_d.tensor, offset=wd_d.offset + h * 384, ap=[[1, 128], [128, 2], [1, 128]])
            dma("q_act", XT[:, h, :, :], src, reads=["wd"], writes=["XT"])
        for i, gt in enumerate((gq4m, gq4f)):
            nm = (negMm, negMf)[i]
            add("dve", lambda e, gt=gt: e.tensor_reduce(out=gmx[:, 0:4], in_=gt[:, :, :], axis=AX.X, op=ALU.max,
                                                         apply_absolute_value=True),
                reads=["g4_%d_%d" % (i, j) for j in range(4)], writes=["gmx"])
            add("dve", lambda e, nm=nm: e.scalar_tensor_tensor(out=nm[:, :], in0=gmx[:, 0:1], scalar=-8.0, in1=gmx[:, 2:3],
                                                                op0=ALU.mult, op1=ALU.mult),
                reads=["gmx"], writes=["negM%d" % i])
        add("dve", lambda e: e.tensor_scalar(out=biasM[:, :], in0=c31b[:, :], scalar1=negMm[:, 0:1], scalar2=0.0, op0=ALU.add, op1=ALU.add),
            reads=["c31b", "negM0"], writes=["biasM"])
        add("pool", lambda e: e.memset(negones[:], -1.0), writes=["negones"])
        for t_, nm_ in ((Re, "Re"), (Ro, "Ro"), (mval, "mval"), (cqtok, "cqtok"), (kmT, "kmT")):
            add("pool", lambda e, t_=t_: e.memset(t_[:], 0.0), writes=[nm_])

        slab_seq = []
        slab_issued = [0]
        slab_base = [0]

        def slab_issue(k):
            kind, arg = slab_seq[k]
            slot = k % NSLAB
            sl = slabs[slot]
            key = "slab%d" % slot
            lane = "slab%d" % slot
            if kind == "win":
                g = arg
                base = (0 if g < 4 else 1536) + 128 * (g % 4)
                v = sl[:, 0:8 * 392].rearrange("p (k n) -> p k n", n=392)
                pieces = [(v[:, :, 0:128], win_v[:, :, base:base + 128]),
                          (v[:, :, 128:256], win_v[:, :, base + 512:base + 640]),
                          (v[:, :, 256:384], win_v[:, :, base + 1024:base + 1152])]
                if g == 4:
                    pieces.append((v[:, :, 384:392], win_v[:, :, 3072:3080]))
            elif kind == "wout":
                v = sl[:, :].rearrange("p (k n) -> p k n", n=1024)
                pieces = [(v[:, 0:4, :], wout_v[:, 0:4, :]), (v[:, 4:8, :], wout_v[:, 4:8, :])]
            elif kind == "wgate":
                v = sl[:, :].rearrange("p (k n) -> p k n", n=1024)
                pieces = [(v[:, 0:4, :], wgate_v[:, 0:4, :]), (v[:, 4:8, :], wgate_v[:, 4:8, :])]
            elif kind == "wup":
                v = sl[:, :].rearrange("p (k n) -> p k n", n=1024)
                c0 = arg * 1024
                pieces = [(v[:, 0:4, :], wup_v[:, 0:4, c0:c0 + 1024]), (v[:, 4:8, :], wup_v[:, 4:8, c0:c0 + 1024])]
            elif kind == "wdn":
                v = sl[:, :].rearrange("p (j n) -> p j n", n=256)
                c0 = arg * 256
                pieces = [(v[:, 8 * q:8 * q + 8, :], wdn_v[:, 8 * q:8 * q + 8, c0:c0 + 256]) for q in range(4)]
            elif kind == "wproj":
                v = sl[:, 0:2048].rearrange("p (k n) -> p k n", n=1024)
                pieces = [(v[:, :, :], wproj_v[:, :, :])]
            add("q_pool", lambda e, pieces=pieces: [e.dma_start(out=o, in_=i) for o, i in pieces],
                writes=[key], lane=lane, ndma=len(pieces), name="slab%d" % k)

        def use_slab(k, ahead=2):
            while slab_issued[0] <= min(k + ahead, len(slab_seq) - 1):
                slab_issue(slab_issued[0])
                slab_issued[0] += 1
            return k % NSLAB

        for b in range(nb):
            for g in range(8):
                slab_seq.append(("win", g))
            for tb in range(4):
                slab_seq.append(("wout", 0))
                for jg in range(4):
                    slab_seq.append(("wup", jg))
                for nq in range(4):
                    slab_seq.append(("wdn", nq))
                slab_seq.append(("wgate", 0))
                slab_seq.append(("wproj", 0))
        slab_ptr = [0]

        def next_slab(ahead=2):
            k = slab_ptr[0]
            slab_ptr[0] += 1
            return use_slab(k, ahead)

        def rmsnorm_to_T(xtile, xkey, gtile, gkey, dstT, dstkey, col0, junk, junkkey):
            add("act", lambda e: e.activation(out=junk, in_=xtile, func=AF.Square, accum_out=ssx[:, :]),
                reads=[xkey], writes=[junkkey, "ssx"])
            add("act", lambda e: e.activation(out=rsx[:, :], in_=ssx[:, :], func=AF.Ln, bias=EPS, scale=1.0 / D),
                reads=["ssx"], writes=["rsx"])
            add("act", lambda e: e.activation(out=rsx[:, :], in_=rsx[:, :], func=AF.Exp, scale=-0.5),
                reads=["rsx"], writes=["rsx"])
            return None

        def norm_apply(xtile, xkey, gtile, gkey, hbt, hbkey):
            add("dve", lambda e: e.scalar_tensor_tensor(out=hbt, in0=xtile, scalar=rsx[:, 0:1], in1=gtile,
                                                        op0=ALU.mult, op1=ALU.mult),
                reads=[xkey, "rsx", gkey], writes=[hbkey])

        def transpose8(hbt, hbkey, dst, dstkey, eng="act"):
            def f(e):
                r = None
                for k in range(8):
                    r = e.transpose(tp8[:, k, :], hbt[:, k * 128:(k + 1) * 128], ident[:, :])
                return r
            add("pe", f, reads=[hbkey, "ident"], writes=["tp"])
            if eng == "act":
                add("act", lambda e: e.activation(out=dst, in_=tp8[:, :, :], func=AF.Copy), reads=["tp"], writes=[dstkey])
            else:
                add("dve", lambda e: e.tensor_copy(out=dst, in_=tp8[:, :, :]), reads=["tp"], writes=[dstkey])

        def check(tag):
            if stop == tag:
                raise _Stop()

        try:
            check('setup')
            for b in range(nb):
                for VA_ in VAs:
                    add("pool", lambda e, VA_=VA_: e.memset(VA_[:, :, 64:128], 0.0), writes=["VAc"])
                    add("pool", lambda e, VA_=VA_: e.memset(VA_[:, :, 64:65], 1.0), reads=["VAc"], writes=["VAc"])
                add("pool", lambda e: e.memset(qkn2[0][:, :, :], 0.0), writes=["qkn0"])
                add("pool", lambda e: e.memset(qkn2[1][:, :, :], 0.0), writes=["qkn1"])
                pending_norm = []
                qbi = [0]
                sc = [0]

                def flush_norm():
                    while pending_norm:
                        pending_norm.pop(0)()
                add("pool", lambda e: e.memset(ar[64:128, 0:8192], 0.0), writes=[("QAaug", 0, 0), ("QAaug", 0, 1), ("KAaug", 0)])
                add("pool", lambda e: e.memset(ar[64:128, 11264:11264 + 8192], 0.0), writes=[("QAaug", 1, 0), ("QAaug", 1, 1), ("KAaug", 1)])

                dma("q_sp", gbuf[0][:, :], gattn_d.partition_broadcast(128), writes=["gbuf0"], lane="gb0")
                for tt in range(NT):
                    i2 = tt % 2
                    dma("q_sp", xt[i2], x_d[b, tt * 128:(tt + 1) * 128, :], writes=["xt%d" % i2], lane="xt%d" % i2)
                    rmsnorm_to_T(xt[i2], "xt%d" % i2, None, None, None, None, 0, a1junk, "a1junk")
                    norm_apply(xt[i2], "xt%d" % i2, gbuf[0][:, :], "gbuf0", hb[i2], "hb%d" % i2)
                    transpose8(hb[i2], "hb%d" % i2, hTv[:, :, tt * 128:(tt + 1) * 128], ("hT", tt), eng=("act" if tt % 2 else "dve"))
                def inproj_steps(g):
                    steps = []
                    moba = g < 4
                    st_ = g % 2
                    QA, KA, VA = QAs[st_], KAs[st_], VAs[st_]
                    gain4 = gq4m if moba else gq4f
                    gkeys = ["g4_%d_%d" % (0 if moba else 1, j) for j in range(4)]
                    NW = 392 if g == 4 else 384
                    ctx = {}

                    def prologue():
                        slot = next_slab()
                        ctx["skey"] = "slab%d" % slot
                        ctx["wv"] = slabs[slot][:, 0:8 * 392].rearrange("p (k n) -> p k n", n=392)
                        if g in (0, 1, 4, 5):
                            ty = 0 if moba else 1
                            add("q_sp", lambda e: [e.dma_start(out=KA[64:72, 0, :], in_=kaug_d[ty]),
                                                   e.dma_start(out=KA[64:72, 1, :], in_=kaug_d[ty])],
                                writes=[("KAaug", st_)], lane="kaug%d" % st_, ndma=2)
                        if moba:
                            add("pool", lambda e: e.memset(QA[64:72, :, :], 0.0), writes=[("QAaug", st_, 0), ("QAaug", st_, 1)])
                        stage0(0)

                    def stage0(tt):
                        ip = ipp[tt % 2]
                        ipk = "ipp%d" % (tt % 2)
                        cs = slice(tt * 128, (tt + 1) * 128)
                        wv = ctx["wv"]
                        skey = ctx["skey"]

                        def f(e):
                            r = None
                            for k in range(8):
                                r = e.matmul(ip[:, 0:NW], lhsT=hTv[:, k, cs], rhs=wv[:, k, 0:NW], start=(k == 0), stop=(k == 7))
                            return r
                        add("pe", f, reads=[("hT", tt), skey], writes=[ipk])

                    def stage1a(tt):
                        ip = ipp[tt % 2]
                        ipk = "ipp%d" % (tt % 2)
                        d2 = tt % 2
                        add("act", lambda e: e.activation(out=sqs2[d2], in_=ip[:, 0:256], func=AF.Square), reads=[ipk], writes=["sqs%d" % d2])
                        add("dve", lambda e: e.tensor_reduce(out=ssq4[:, d2, :], in_=sqs2[d2].rearrange("p (a c) -> p a c", c=64), axis=AX.X, op=ALU.add),
                            reads=["sqs%d" % d2], writes=["ssq4_%d" % d2])
                        add("dve", lambda e: e.tensor_tensor(out=tq2[d2], in0=ip[:, 0:256],
                                                             in1=gain4[:, :, :].rearrange("p a c -> p (a c)"), op=ALU.mult),
                            reads=[ipk] + gkeys, writes=["tq%d" % d2])
                        add("dve", lambda e: e.tensor_copy(out=VA[:, tt, 0:64], in_=ip[:, 256:320]),
                            reads=[ipk], writes=[("VA", st_, tt, 0)])
                        add("dve", lambda e: e.tensor_copy(out=VA[:, tt, 128:192], in_=ip[:, 320:384]),
                            reads=[ipk], writes=[("VA", st_, tt, 1)])
                        if g == 4:
                            add("dve", lambda e: e.tensor_tensor(out=z8[:, d2, :], in0=ip[:, 384:392], in1=bfb[:, :], op=ALU.add),
                                reads=[ipk, "bfb"], writes=["z8_%d" % d2])

                    def stage1b(tt):
                        d2 = tt % 2
                        add("act", lambda e: e.activation(out=rs4[:, d2, :], in_=ssq4[:, d2, :], func=AF.Ln, bias=EPS, scale=1.0 / 64),
                            reads=["ssq4_%d" % d2], writes=["rs4_%d" % d2])
                        add("act", lambda e: e.activation(out=rs4[:, d2, :], in_=rs4[:, d2, :], func=AF.Exp, scale=-0.5),
                            reads=["rs4_%d" % d2], writes=["rs4_%d" % d2])
                        add("pool", lambda e: e.tensor_tensor(out=qkn2[d2][:, :, 0:64], in0=tq2[d2].rearrange("p (a c) -> p a c", c=64),
                                                              in1=rs4[:, d2, :].unsqueeze(2).to_broadcast([128, 4, 64]), op=ALU.mult),
                            reads=["tq%d" % d2, "rs4_%d" % d2], writes=["qkn%d" % d2])
                        if g == 4:
                            add("act", lambda e: e.activation(out=z8[:, d2, :], in_=z8[:, d2, :], func=AF.Exp, scale=-1.0),
                                reads=["z8_%d" % d2], writes=["z8_%d" % d2])
                            add("act", lambda e: e.activation(out=l8[:, d2, :], in_=z8[:, d2, :], func=AF.Ln, bias=1.0, scale=1.0),
                                reads=["z8_%d" % d2], writes=["l8_%d" % d2])

                    def stage2_pe(tt):
                        d2 = tt % 2

                        def ft(e):
                            r = None
                            for a in range(4):
                                r = e.transpose(tp8[:, a, :], qkn2[d2][:, a, :], ident[:, :])
                            return r
                        add("pe", ft, reads=["qkn%d" % d2, "ident"], writes=["tp"])

                    def stage2_rest(tt):
                        d2 = tt % 2
                        cs = slice(tt * 128, (tt + 1) * 128)
                        add("dve", lambda e: e.tensor_copy(out=QA[0:64, :, cs], in_=tp8[0:64, 0:2, :]),
                            reads=["tp"], writes=[("QA", st_, tt)])
                        add("dve", lambda e: e.tensor_copy(out=KA[0:64, :, cs], in_=tp8[0:64, 2:4, :]),
                            reads=["tp"], writes=[("KA", st_, tt)])
                        if g == 4:
                            def fc(e):
                                r = e.matmul(CUp, lhsT=utri[:, :], rhs=l8[:, d2, :], start=True, stop=(tt == 0))
                                if tt > 0:
                                    r = e.matmul(CUp, lhsT=onesf[:, :], rhs=pref8[:, :], start=False, stop=True)
                                return r
                            add("pe", fc, reads=["l8_%d" % d2, "pref8", "utri", "onesf"], writes=["tp"])
                            add("dve", lambda e: e.tensor_scalar(out=ncb[:, tt, :], in0=CUp, scalar1=negMf[:, 0:1], scalar2=0.0, op0=ALU.add, op1=ALU.add),
                                reads=["tp", "negM1"], writes=[("ncb", tt)])
                            add("dve", lambda e: e.tensor_scalar(out=cqtok[:, 0:8], in0=CUp, scalar1=-8.0, scalar2=1.0, op0=ALU.mult, op1=ALU.mult),
                                reads=["tp", "cqtok"], writes=["cqtok"])
                            if tt == 0:
                                add("dve", lambda e: e.tensor_copy(out=pref8[:, :], in_=l8[:, d2, :]), reads=["l8_%d" % d2], writes=["pref8"])
                            else:
                                add("dve", lambda e: e.tensor_tensor(out=pref8[:, :], in0=pref8[:, :], in1=l8[:, d2, :], op=ALU.add),
                                    reads=["l8_%d" % d2, "pref8"], writes=["pref8"])
                            add("pe", lambda e: e.transpose(tp8[:, 4, :], cqtok[:, :], ident[:, :]), reads=["cqtok", "ident"], writes=["tp"])
                            add("act", lambda e: e.activation(out=cqT[0:8, cs], in_=tp8[0:8, 4, :], func=AF.Copy),
                                reads=["tp"], writes=["msT"])

                    steps.append(prologue)
                    for n_ in range(1, NT + 3):
                        def it(n_=n_):
                            if IPIPE == 4:
                                if 0 <= n_ - 3 < NT:
                                    stage2_pe(n_ - 3)
                                if n_ < NT:
                                    stage0(n_)
                                if 0 <= n_ - 1 < NT:
                                    stage1a(n_ - 1)
                                if 0 <= n_ - 2 < NT:
                                    stage1b(n_ - 2)
                                if 0 <= n_ - 3 < NT:
                                    stage2_rest(n_ - 3)
                            else:
                                t_ = n_ - 1
                                if t_ == 0:
                                    stage1a(0)
                                    stage1b(0)
                                if t_ < NT:
                                    if t_ + 1 < NT:
                                        stage0(t_ + 1)
                                        stage1a(t_ + 1)
                                        stage1b(t_ + 1)
                                    stage2_pe(t_)
                                    stage2_rest(t_)
                        steps.append(it)
                    if moba:
                        def kmstep():
                            for hl in range(2):
                                add("dve", lambda e, hl=hl: e.tensor_reduce(out=km32[:, hl, :], in_=KA[0:64, hl, :].rearrange("p (n c) -> p n c", c=256),
                                                                            axis=AX.X, op=ALU.add),
                                    reads=[("KA", st_, t_) for t_ in range(NT)], writes=["km32"])
                            add("dve", lambda e: e.tensor_scalar(out=kmT[0:64, :, :], in0=km32[:, :, :], scalar1=1.0 / 256, scalar2=1.0, op0=ALU.mult, op1=ALU.mult),
                                reads=["km32", "kmT"], writes=["kmT"])
                        steps.append(kmstep)
                        def m_a(tt):
                            own = tt // 2
                            m2 = tt % 2
                            cs = slice(tt * 128, (tt + 1) * 128)
                            Gv = Gp2[m2]

                            def fg(e):
                                r = None
                                for hl in range(2):
                                    r = e.matmul(Gv[:, hl * 8:hl * 8 + 8], lhsT=QA[:, hl, cs], rhs=kmT[:, hl, :], start=True, stop=True)
                                return r
                            add("pe", fg, reads=[("QA", st_, tt), ("QAaug", st_, 0), ("QAaug", st_, 1), "kmT"], writes=["tp"])
                            add("pool", lambda e: e.memset(gsb2[m2][:, :, :], -1e30), writes=[("gsb", m2)])
                            add("pool", lambda e: e.memset(mval2[m2][:, 0:16], 0.0), writes=[("mval", m2)])
                            add("dve", lambda e: e.tensor_copy(out=gsb2[m2][:, :, 0:own],
                                                               in_=Gv.rearrange("p (h n) -> p h n", n=8)[:, :, 0:own]),
                                reads=["tp", ("gsb", m2)], writes=[("gsb", m2)])
                            for hl in range(2):
                                add("dve", lambda e, hl=hl: e.max(out=top8b[m2][:, hl, :], in_=gsb2[m2][:, hl, :]), reads=[("gsb", m2)], writes=[("top8", m2, hl)])
                                add("dve", lambda e, hl=hl: e.tensor_scalar(out=mval2[m2][:, hl * 8:hl * 8 + own], in0=gsb2[m2][:, hl, 0:own],
                                                                            scalar1=top8b[m2][:, hl, 2:3], scalar2=1.0, op0=ALU.is_lt, op1=ALU.mult),
                                    reads=[("gsb", m2), ("top8", m2, hl), ("mval", m2)], writes=[("mval", m2)])

                        def m_b(tt):
                            m2 = tt % 2
                            cs = slice(tt * 128, (tt + 1) * 128)
                            add("pe", lambda e: e.transpose(tp8[:, 5, :], mval2[m2][:, :], ident[:, :]), reads=[("mval", m2), "ident"], writes=["tp"])
                            add("act", lambda e: e.activation(out=msT[0:16, cs], in_=tp8[0:16, 5, :], func=AF.Copy),
                                reads=["tp"], writes=["msT"])

                        for tt in range(8, NT + 1):
                            def mstep(tt=tt):
                                if tt - 1 >= 8:
                                    m_b(tt - 1)
                                if tt < NT:
                                    m_a(tt)
                            steps.append(mstep)

                        def augdma():
                            for hl in range(2):
                                dma("q_sp", QA[64:72, hl, 1024:2048], msT[hl * 8:hl * 8 + 8, 1024:2048], reads=["msT"],
                                    writes=[("QAaug", st_, hl)], lane="qaug%d_%d" % (st_, hl))
                        steps.append(augdma)
                    else:
                        def augdma():
                            for hl in range(2):
                                hf = 2 * (g - 4) + hl
                                dma("q_sp", QA[64:65, hl, :], cqT[hf:hf + 1, :], reads=["msT"], writes=[("QAaug", st_, hl)],
                                    lane="qaug%d_%d" % (st_, hl))
                        steps.append(augdma)
                    return steps

                def attn_steps(g):
                    moba = g < 4
                    st_ = g % 2
                    QA, KA, VA = QAs[st_], KAs[st_], VAs[st_]
                    ents = []
                    for hl in range(2):
                        h = 2 * g + hl
                        hm = h if moba else h - 8
                        for qb in range(4):
                            nkt = 4 * (qb + 1)
                            ob = qbi[0] % 2
                            qbi[0] += 1
                            for kt in range(nkt):
                                j = kt - 4 * qb
                                c0 = 128 * j if j >= 0 else 0
                                extras = []
                                if moba:
                                    if j >= 0:
                                        extras.append((c0, XT[:, hm, 1, :]))
                                        if j < 3:
                                            extras.append((c0 + 128, XT[:, hm, 0, :]))
                                    elif j == -1:
                                        extras.append((0, XT[:, hm, 0, :]))
                                else:
                                    if j >= 0:
                                        extras.append((c0, cmx[:, :]))
                                bi = sc[0] % 3
                                sc[0] += 1
                                ents.append(dict(hl=hl, hm=hm, qb=qb, kt=kt, nkt=nkt, q0=qb * 512, c0=c0, extras=extras, bi=bi,
                                                 Ob=Ops[ob], okey="O%d" % ob))
                    n = len(ents)

                    def do_S(i):
                        en = ents[i]
                        bi, c0, extras, hl, q0, kt, qb = en["bi"], en["c0"], en["extras"], en["hl"], en["q0"], en["kt"], en["qb"]
                        ks = slice(kt * 128, (kt + 1) * 128)

                        def fs(e):
                            r = e.matmul(Sps[bi][:, c0:512], lhsT=KA[:, hl, ks], rhs=QA[:, hl, q0 + c0:q0 + 512],
                                         start=True, stop=(len(extras) == 0))
                            for n_, (cc, lt) in enumerate(extras):
                                r = e.matmul(Sps[bi][:, cc:cc + 128], lhsT=lt, rhs=antij[:, :], start=False,
                                             stop=(n_ == len(extras) - 1))
                            return r
                        add("pe", fs, reads=[("KA", st_, kt), ("KAaug", st_), ("QAaug", st_, hl), "XT", "cmx", "antij"]
                            + [("QA", st_, 4 * qb + t_) for t_ in range(4)], writes=["S%d" % bi])

                    def do_E(i):
                        en = ents[i]
                        bi, c0, hm, kt = en["bi"], en["c0"], en["hm"], en["kt"]
                        if moba:
                            bias_ap = biasM[:, hm:hm + 1]
                            bkeys = ["biasM"]
                        else:
                            bias_ap = ncb[:, kt, hm:hm + 1]
                            bkeys = [("ncb", kt)]
                        add("act", lambda e: e.activation(out=Pt[bi][:, c0:512], in_=Sps[bi][:, c0:512],
                                                          func=AF.Exp, bias=bias_ap, scale=SCALE),
                            reads=["S%d" % bi] + bkeys, writes=["P%d" % bi])

                    def do_PV(i):
                        en = ents[i]
                        bi, c0, hl, kt, nkt, Ob, okey, qb, q0 = (en["bi"], en["c0"], en["hl"], en["kt"], en["nkt"], en["Ob"], en["okey"],
                                                                 en["qb"], en["q0"])

                        def fpv(e):
                            if hl == 0:
                                return e.matmul(Ob[:, c0:512], lhsT=VA[:, kt, 0:128], rhs=Pt[bi][:, c0:512],
                                                start=(kt == 0), stop=(kt == nkt - 1))
                            return e.matmul(Ob[:, c0:512], lhsT=VA[:, kt, 64:192], rhs=Pt[bi][:, c0:512],
                                            start=(kt == 0), stop=(kt == nkt - 1))
                        add("pe", fpv, reads=["P%d" % bi, ("VA", st_, kt, hl), "VAc"], writes=[okey])
                        if kt == nkt - 1:
                            ocs = slice(q0, q0 + 512)
                            if hl == 0:
                                Rt, rk, row, sel, selk, rows = Re, "Re", slice(64, 65), sele, "sele", slice(0, 64)
                            else:
                                Rt, rk, row, sel, selk, rows = Ro, "Ro", slice(0, 1), selo, "selo", slice(64, 128)
                            add("act", lambda e: e.activation(out=den32[row, :], in_=Ob[row, :], func=AF.Ln), reads=[okey], writes=["den32"])
                            add("act", lambda e: e.activation(out=Rt[row, :], in_=den32[row, :], func=AF.Exp, scale=-1.0),
                                reads=["den32", rk], writes=[rk])

                            def norm():
                                add("pe", lambda e: e.matmul(BCv, lhsT=sel[:, :], rhs=Rt[:, :], start=True, stop=True),
                                    reads=[rk, selk], writes=["tp"])
                                add("dve", lambda e: e.tensor_copy(out=bcs[rows, :], in_=BCv[rows, :]), reads=["tp"], writes=["bcs"])
                                add("dve", lambda e: e.tensor_tensor(out=OTv[rows, g, ocs], in0=Ob[rows, :], in1=bcs[rows, :], op=ALU.mult),
                                    reads=[okey, "bcs"], writes=[("OT", g, qb, hl)] + OT_ALIAS.get(g, []))
                            pending_norm.append(norm)

                    steps = []
                    for i in range(n):
                        def step(i=i):
                            if i == 0:
                                do_S(0)
                            if i + 1 < n:
                                do_S(i + 1)
                            do_E(i)
                            if i > 0:
                                do_PV(i - 1)
                            if ents[i]["kt"] == 3:
                                flush_norm()
                            if i == n - 1:
                                do_PV(i)
                                flush_norm()
                        steps.append(step)
                    return steps

                for f_ in inproj_steps(0):
                    f_()
                check('ip0')
                for g in range(8):
                    A_ = attn_steps(g)
                    I_ = inproj_steps(g + 1) if g < 7 else []
                    ni = 0
                    for ia, a_ in enumerate(A_):
                        a_()
                        tgt = ((ia + 1) * len(I_)) // len(A_)
                        while ni < tgt:
                            I_[ni]()
                            ni += 1
                    while ni < len(I_):
                        I_[ni]()
                        ni += 1
                    flush_norm()
                    check('g%d' % g)
                    check('attn%d' % g)
                P.barrier()
                check('B')

                dma("q_sp", gbuf[0][:, :], gmlp_d.partition_broadcast(128), writes=["gbuf0"], lane="gb0")
                dma("q_sp", gbuf[1][:, :], gple_d.partition_broadcast(128), writes=["gbuf1"], lane="gb1")
                junkC = ar[:, 13312:15360].bitcast(F32)
                tpS2 = Sps[2][:, :].bitcast(BF16).rearrange("p (k s) -> p k s", k=8)

                def c_norm(i, par, gt, gk):
                    xk = "xc%d" % i
                    add("act", lambda e: e.activation(out=junkC, in_=xc[i], func=AF.Square, accum_out=ssx2[:, par:par + 1]),
                        reads=[xk], writes=["r32_0", "r32_1", ("ssx", par)])
                    add("act", lambda e: e.activation(out=rsx2[:, par:par + 1], in_=ssx2[:, par:par + 1], func=AF.Ln, bias=EPS, scale=1.0 / D),
                        reads=[("ssx", par)], writes=[("rsx", par)])
                    add("act", lambda e: e.activation(out=rsx2[:, par:par + 1], in_=rsx2[:, par:par + 1], func=AF.Exp, scale=-0.5),
                        reads=[("rsx", par)], writes=[("rsx", par)])
                    add("dve", lambda e: e.scalar_tensor_tensor(out=hbC2[par], in0=xc[i], scalar=rsx2[:, par:par + 1], in1=gt,
                                                                op0=ALU.mult, op1=ALU.mult),
                        reads=[xk, ("rsx", par), gk], writes=[("hbC", par)])

                def c_tr(i, par):
                    def f(e):
                        r = None
                        for k in range(8):
                            r = e.transpose(tp8[:, k, :], hbC2[par][:, k * 128:(k + 1) * 128], ident[:, :])
                        return r
                    add("pe", f, reads=[("hbC", par), "ident"], writes=["tp"])
                    add("act", lambda e: e.activation(out=h2T[:, :, i * 128:(i + 1) * 128], in_=tp8[:, :, :], func=AF.Copy),
                        reads=["tp"], writes=[("h2T", i)])

                for tb in range(4):
                    s_out = next_slab()
                    wo = slabs[s_out][:, :].rearrange("p (k n) -> p k n", n=1024)
                    PPo = [(pp01, ["pp01", "pp01h0", "pp01h1"]), (pp67, ["pp67"])]

                    def op_s0(i, tb=tb, wo=wo, s_out=s_out):
                        tt = 4 * tb + i
                        cs = slice(tt * 128, (tt + 1) * 128)
                        pp_, pk_ = PPo[i % 2]
                        dma("q_sp", xc[i], x_d[b, cs, :], writes=["xc%d" % i], lane="xc%d" % i)

                        def fo(e):
                            r = None
                            for n in range(2):
                                for k in range(8):
                                    r = e.matmul(pp_[:, n * 512:(n + 1) * 512], lhsT=OTv[:, k, cs], rhs=wo[:, k, n * 512:(n + 1) * 512],
                                                 start=(k == 0), stop=(k == 7))
                            return r
                        add("pe", fo, reads=["slab%d" % s_out] + [("OT", k, tt // 4, h_) for k in range(8) for h_ in range(2)], writes=pk_)

                    def op_s1(i):
                        pp_, pk_ = PPo[i % 2]
                        xk = "xc%d" % i
                        add("dve", lambda e: e.tensor_tensor(out=xc[i], in0=pp_[:, :], in1=xc[i], op=ALU.add), reads=pk_ + [xk], writes=[xk])
                        c_norm(i, i % 2, gbuf[0][:, :], "gbuf0")

                    op_s0(0)
                    for i in range(4):
                        if i + 1 < 4:
                            op_s0(i + 1)
                        op_s1(i)
                        c_tr(i, i % 2)
                    for jg in range(4):
                        s_up = next_slab()
                        wu = slabs[s_up][:, :].rearrange("p (k n) -> p k n", n=1024)
                        for jj in range(8):
                            j = 8 * jg + jj
                            bi = j % 3
                            ri = j % 2

                            def fu(e, bi=bi, jj=jj, wu=wu):
                                r = None
                                for k in range(8):
                                    r = e.matmul(Sps[bi][:, :], lhsT=wu[:, k, jj * 128:(jj + 1) * 128], rhs=h2T[:, k, :], start=(k == 0), stop=(k == 7))
                                return r
                            add("pe", fu, reads=["slab%d" % s_up] + [("h2T", i) for i in range(4)], writes=["S%d" % bi])
                            add("act", lambda e, bi=bi, ri=ri: e.activation(out=r32[ri], in_=Sps[bi][:, :], func=AF.Relu), reads=["S%d" % bi], writes=["r32_%d" % ri])
                            add("dve", lambda e, ri=ri, j=j: e.tensor_tensor(out=uTv[:, j, :], in0=r32[ri], in1=r32[ri], op=ALU.mult),
                                reads=["r32_%d" % ri], writes=[("uT", j)])
                    cnt = 0
                    for nq in range(4):
                        s_dn = next_slab()
                        wdv = slabs[s_dn][:, :].rearrange("p (j n) -> p j n", n=256)
                        for i in range(4):
                            hb_ = cnt % 2
                            cnt += 1
                            xk = "xc%d" % i

                            def fd(e, hb_=hb_, i=i, wdv=wdv):
                                r = None
                                for j in range(32):
                                    r = e.matmul(pp01[:, hb_ * 512:hb_ * 512 + 256], lhsT=uTv[:, j, i * 128:(i + 1) * 128], rhs=wdv[:, j, :],
                                                 start=(j == 0), stop=(j == 31))
                                return r
                            add("pe", fd, reads=["slab%d" % s_dn] + [("uT", j) for j in range(32)], writes=["pp01h%d" % hb_])
                            add("dve", lambda e, hb_=hb_, i=i, nq=nq: e.tensor_tensor(out=xc[i][:, nq * 256:(nq + 1) * 256],
                                                                                      in0=pp01[:, hb_ * 512:hb_ * 512 + 256],
                                                                                      in1=xc[i][:, nq * 256:(nq + 1) * 256], op=ALU.add),
                                reads=["pp01h%d" % hb_, xk], writes=[xk])
                    s_g = next_slab()
                    s_p = next_slab(ahead=1)
                    wg = slabs[s_g][:, :].rearrange("p (k n) -> p k n", n=1024)
                    wp = slabs[s_p][:, 0:2048].rearrange("p (k n) -> p k n", n=1024)
                    GO = [([pp01[:, 0:512], pp01[:, 512:1024]], ["pp01", "pp01h0", "pp01h1"]), ([Sps[0][:, :], Sps[1][:, :]], ["S0", "S1"])]

                    def ple_a(i, tb=tb):
                        tt = 4 * tb + i
                        cs = slice(tt * 128, (tt + 1) * 128)
                        par = i % 2
                        c_norm(i, par, gbuf[1][:, :], "gbuf1")
                        dma("q_sp", pf2[par], p_d[b, cs, :], writes=[("pf", par)], lane="pf%d" % par)
                        add("dve", lambda e: e.tensor_copy(out=pb2[par], in_=pf2[par]), reads=[("pf", par)], writes=[("pb", par)])

                    def ple_b(i):
                        par = i % 2
                        c_tr(i, par)

                        def fpt(e):
                            e.transpose(tpS2[:, 0, :], pb2[par][:, 0:128], ident[:, :])
                            return e.transpose(tpS2[:, 1, :], pb2[par][:, 128:256], ident[:, :])
                        add("pe", fpt, reads=[("pb", par), "ident"], writes=["S2"])
                        add("dve", lambda e: e.tensor_copy(out=pT[:, :, i * 128:(i + 1) * 128], in_=tpS2[:, 0:2, :]), reads=["S2"], writes=[("pT", i)])

                    def ple_c(i, wg=wg, wp=wp, s_g=s_g, s_p=s_p):
                        gb_, gk_ = GO[i % 2]

                        def fg2(e):
                            r = None
                            for n in range(2):
                                for k in range(8):
                                    r = e.matmul(gb_[n], lhsT=h2T[:, k, i * 128:(i + 1) * 128], rhs=wg[:, k, n * 512:(n + 1) * 512],
                                                 start=(k == 0), stop=(k == 7))
                            return r
                        add("pe", fg2, reads=["slab%d" % s_g, ("h2T", i)], writes=gk_)

                        def fp2(e):
                            r = None
                            for n in range(2):
                                for k in range(2):
                                    r = e.matmul(pp67[:, n * 512:(n + 1) * 512], lhsT=pT[:, k, i * 128:(i + 1) * 128], rhs=wp[:, k, n * 512:(n + 1) * 512],
                                                 start=(k == 0), stop=(k == 1))
                            return r
                        add("pe", fp2, reads=["slab%d" % s_p, ("pT", i)], writes=["pp67"])

                    def ple_d(i, tb=tb):
                        tt = 4 * tb + i
                        cs = slice(tt * 128, (tt + 1) * 128)
                        xk = "xc%d" % i
                        gb_, gk_ = GO[i % 2]
                        for n in range(2):
                            add("act", lambda e, n=n: e.activation(out=sig[:, n * 512:(n + 1) * 512], in_=gb_[n], func=AF.Sigmoid),
                                reads=gk_, writes=["sig%d" % n])
                        add("dve", lambda e: e.tensor_tensor(out=tmpf, in0=pp67[:, :], in1=sig, op=ALU.mult), reads=["pp67", "sig0", "sig1", "tmpf"], writes=["tmpf"])
                        add("dve", lambda e: e.tensor_tensor(out=xc[i], in0=xc[i], in1=tmpf, op=ALU.add), reads=["tmpf", xk], writes=[xk])
                        dma("q_sp", y_d[b, cs, :], xc[i], reads=[xk], lane="out%d" % i)

                    for fn_, i_ in ((ple_a, 0), (ple_a, 1), (ple_b, 0), (ple_c, 0), (ple_b, 1), (ple_d, 0), (ple_c, 1), (ple_a, 2),
                                    (ple_b, 2), (ple_d, 1), (ple_c, 2), (ple_a, 3), (ple_b, 3), (ple_d, 2), (ple_c, 3), (ple_d, 3)):
                        fn_(i_)
                P.barrier()


        except _Stop:
            P.barrier()
            src = dbg(locals()) if dbg is not None else ident[:, :]
            dma('q_sp', dbg_d[0:src.shape[0], 0:src.shape[1]], src, lane='dbgout')
        P.finalize(st)
        fin = [(("lane", l), P.lane_cnt[l]) for l in ["out%d" % i for i in range(4)] + ["dbgout"] if l in P.lane_cnt]
        with nc.allow_low_precision(reason="bf16 matmul operands, fp32 accumulation"), nc.Block() as block:
            P.emit(block, final_waits=fin)
        build_program.stats = (len(P.ops), P.nwaits)
    return nc


_CACHE = {}


def kernel(**inputs):
    consts = _consts()
    if "nc" not in _CACHE:
        _CACHE["nc"] = build_program(NB)
    nc = _CACHE["nc"]
    f32 = lambda a: np.ascontiguousarray(np.asarray(a, dtype=np.float32))
    x = f32(inputs["x"])
    p = f32(inputs["p"])[0]
    shared = {
        "rel_bias": f32(inputs["rel_bias"]),
        "g_attn": f32(inputs["g_attn"]).reshape(1, D),
        "w_in": f32(inputs["w_in"])[0],
        "b_f": f32(inputs["b_f"]).reshape(1, 8),
        "gq_moba": f32(inputs["gq_moba"]).reshape(1, 64),
        "gk_moba": f32(inputs["gk_moba"]).reshape(1, 64),
        "gq_fox": f32(inputs["gq_fox"]).reshape(1, 64),
        "gk_fox": f32(inputs["gk_fox"]).reshape(1, 64),
        "w_out": f32(inputs["w_out"])[0],
        "g_mlp": f32(inputs["g_mlp"]).reshape(1, D),
        "w_up": f32(inputs["w_up"])[0],
        "w_down": f32(inputs["w_down"])[0],
        "g_ple": f32(inputs["g_ple"]).reshape(1, D),
        "w_ple_gate": f32(inputs["w_ple_gate"])[0],
        "w_ple_proj": f32(inputs["w_ple_proj"])[0],
    }
    shared.update(consts)
    in_maps = []
    for c in range(NCORES):
        m = dict(shared)
        m["x"] = np.ascontiguousarray(x[c * NB:(c + 1) * NB])
        m["p"] = np.ascontiguousarray(p[c * NB:(c + 1) * NB])
        in_maps.append(m)
    res = run_bass_kernel_spmd(nc, in_maps, core_ids=list(range(NCORES)))
    out = np.concatenate([np.asarray(r["y"], dtype=np.float32) for r in res.results], axis=0)
    return out
```

```python
import contextlib
import numpy as np
import ml_dtypes
import concourse.bass as bass
import concourse.mybir as mybir
from concourse.bass_utils import run_bass_kernel_spmd

F32 = mybir.dt.float32
BF16 = mybir.dt.bfloat16
AF = mybir.ActivationFunctionType
ALU = mybir.AluOpType
AX = mybir.AxisListType

S = 2048
D = 1024
NT = S // 128
EPS = 1e-6
SCALE = 0.125
NCORES = 8
NBATCH = 16
NB = NBATCH // NCORES
NSLAB = 3
OT_ALIAS = {0: ["xt0"], 1: ["xt1"], 2: ["hb0", "hb1"], 3: ["a1junk"]}
IPIPE = 4


class Op:
    __slots__ = ("eng", "fn", "reads", "writes", "idx", "deps", "sig", "signum",
                 "waits", "lane", "ndma", "dmacum", "know", "name")


class Prog:
    STREAMS = ("pe", "act", "dve", "pool", "sp")

    def __init__(self, nc):
        self.nc = nc
        self.ops = []
        self.last_w = {}
        self.readers = {}
        self.lane_cnt = {}
        self.lane_lastop = {}
        self.last_on = {}

    @staticmethod
    def _stream(eng):
        return {"q_sp": "sp", "q_act": "act", "q_pool": "pool"}.get(eng, eng)

    def add(self, eng, fn, reads=(), writes=(), lane=None, ndma=1, name="", after=()):
        op = Op()
        op.eng, op.fn, op.reads, op.writes = eng, fn, tuple(reads), tuple(writes)
        op.idx = len(self.ops)
        op.lane, op.ndma, op.name = lane, ndma, name
        op.sig, op.signum, op.waits, op.dmacum, op.know = False, 0, {}, 0, None
        deps = {}
        for r in op.reads:
            lw = self.last_w.get(r)
            if lw is not None:
                deps[lw.idx] = (lw, "raw")
        for w in op.writes:
            lw = self.last_w.get(w)
            if lw is not None and lw.idx not in deps:
                deps[lw.idx] = (lw, "waw")
            for rd in self.readers.get(w, ()):
                if rd.idx not in deps:
                    deps[rd.idx] = (rd, "war")
        for a in after:
            deps[a.idx] = (a, "raw")
        out = []
        seng = self._stream(eng)
        for d, kind in deps.values():
            if d is op:
                continue
            if d.lane is None and self._stream(d.eng) == seng:
                if seng == "pe":
                    continue
            out.append(d)
        op.deps = out
        for r in op.reads:
            self.readers.setdefault(r, []).append(op)
        for w in op.writes:
            self.last_w[w] = op
            self.readers[w] = []
        if lane is not None:
            c = self.lane_cnt.get(lane, 0) + 16 * ndma
            self.lane_cnt[lane] = c
            op.dmacum = c
            self.lane_lastop[lane] = op
        else:
            self.last_on[seng] = op
        self.ops.append(op)
        return op

    def barrier(self):
        deps = [o for o in self.last_on.values()] + list(self.lane_lastop.values())
        b = self.add("sp", lambda e: e.nop(), after=deps, name="barrier")
        for s in ("pe", "act", "dve", "pool"):
            self.add(s, lambda e: e.nop(), after=[b], name="barrier_w")
        self.last_w.clear()
        self.readers.clear()

    def finalize(self, stack):
        nc = self.nc
        for op in self.ops:
            for d in op.deps:
                d.sig = True
        cnt = {}
        for op in self.ops:
            if op.lane is None and op.sig:
                s = self._stream(op.eng)
                cnt[s] = cnt.get(s, 0) + 1
                op.signum = cnt[s]
        self.sems = {}
        for s in self.STREAMS:
            self.sems[s] = stack.enter_context(nc.semaphore("sem_" + s))
        for ln in self.lane_cnt:
            self.sems[("lane", ln)] = stack.enter_context(nc.semaphore("dl_%s" % (ln,)))
        know = {s: {} for s in self.STREAMS}
        lane_prev = {}
        nw = 0
        for op in self.ops:
            s = self._stream(op.eng)
            K = know[s]
            need = {}
            dk = []
            for d in op.deps:
                if d.lane is not None:
                    key, val = ("lane", d.lane), d.dmacum
                else:
                    key, val = self._stream(d.eng), d.signum
                dk.append((d, key, val))
                if K.get(key, 0) < val and need.get(key, 0) < val:
                    need[key] = val
            for d, key, val in dk:
                if key in need:
                    for k2, v2 in d.know.items():
                        if K.get(k2, 0) < v2:
                            K[k2] = v2
            for key, val in need.items():
                if K.get(key, 0) < val:
                    K[key] = val
            op.waits = need
            nw += len(need)
            kk = dict(K)
            if op.lane is not None:
                prev = lane_prev.get(op.lane, 0)
                assert K.get(("lane", op.lane), 0) >= prev, "DMA lane %s reuse hazard at %s" % (op.lane, op.name)
                lane_prev[op.lane] = op.dmacum
                kk[("lane", op.lane)] = op.dmacum
            elif op.sig:
                kk[s] = max(kk.get(s, 0), op.signum)
            op.know = kk
        self.nwaits = nw
        return cnt

    def emit(self, block, final_waits=()):
        streams = {s: [] for s in self.STREAMS}
        for op in self.ops:
            streams[self._stream(op.eng)].append(op)
        sems = self.sems

        def run(e, lst, extra=None):
            for op in lst:
                for key, val in op.waits.items():
                    e.wait_ge(sems[key], val)
                r = op.fn(e)
                if op.lane is not None:
                    if not isinstance(r, (list, tuple)):
                        r = [r]
                    assert len(r) == op.ndma, (op.name, len(r), op.ndma)
                    for ins in r:
                        ins.then_inc(sems[("lane", op.lane)], 16)
                elif op.sig:
                    r.then_inc(sems[self._stream(op.eng)], 1)
            if extra:
                for key, val in extra:
                    e.wait_ge(sems[key], val)

        @block.tensor
        def _(e):
            run(e, streams["pe"])

        @block.scalar
        def _(e):
            run(e, streams["act"])

        @block.vector
        def _(e):
            run(e, streams["dve"])

        @block.gpsimd
        def _(e):
            run(e, streams["pool"])

        @block.sync
        def _(e):
            run(e, streams["sp"], extra=final_waits)


def _t5_bucket_np(rel):
    rel = np.maximum(rel, 0)
    relf = np.maximum(rel, 16).astype(np.float32)
    v = (np.log(relf / np.float32(16.0)).astype(np.float32) / np.float32(np.log(128 / 16))).astype(np.float32) * np.float32(16.0)
    large = 16 + v.astype(np.int32)
    large = np.minimum(large, 31)
    return np.where(rel < 16, rel, large)


def _consts():
    bf = ml_dtypes.bfloat16
    c = {}
    c["c_ident"] = np.eye(128, dtype=np.float32).astype(bf)
    c["c_antij"] = np.eye(128, dtype=np.float32)[::-1].copy().astype(bf)
    u = np.arange(128)
    c["c_cmx"] = np.where(u[:, None] + u[None, :] <= 127, 0.0, -65536.0).astype(np.float32).astype(bf)
    se = np.zeros((128, 128), np.float32); se[64, :] = 1.0
    so = np.zeros((128, 128), np.float32); so[0, :] = 1.0
    c["c_sele"] = se.astype(bf)
    c["c_selo"] = so.astype(bf)
    c["c_utri"] = (u[:, None] <= u[None, :]).astype(np.float32)
    c["c_onesf"] = np.ones((128, 128), np.float32)
    oh = np.zeros((33, 384), np.float32)
    for k in range(384):
        dist = 255 - k
        if dist >= 0:
            oh[int(_t5_bucket_np(np.array([dist]))[0]), k] += 1.0
            oh[31, k] -= 1.0
        else:
            oh[32, k] = 1.0
    c["c_ohp"] = oh
    ka = np.zeros((2, 8, S), np.float32)
    for n in range(8):
        ka[0, n, n * 256:(n + 1) * 256] = -65536.0
    ka[1, 0, :] = 1.0
    c["c_kaug"] = ka.astype(bf)
    return c


class _Stop(Exception):
    pass


def build_program(nb=NB, stop=None, dbg=None):
    nc = bass.Bass("TRN2", target_bir_lowering=False)

    def din(name, shape, dt=F32):
        return nc.dram_tensor(name, list(shape), dt, kind="ExternalInput").ap()

    x_d = din("x", [nb, S, D])
    p_d = din("p", [nb, S, 256])
    rel_d = din("rel_bias", [32, 8])
    gattn_d = din("g_attn", [1, D])
    win_d = din("w_in", [D, 3080])
    bf_d = din("b_f", [1, 8])
    gqm_d = din("gq_moba", [1, 64])
    gkm_d = din("gk_moba", [1, 64])
    gqf_d = din("gq_fox", [1, 64])
    gkf_d = din("gk_fox", [1, 64])
    wout_d = din("w_out", [D, D])
    gmlp_d = din("g_mlp", [1, D])
    wup_d = din("w_up", [D, 4096])
    wdn_d = din("w_down", [4096, D])
    gple_d = din("g_ple", [1, D])
    wgate_d = din("w_ple_gate", [D, D])
    wproj_d = din("w_ple_proj", [256, D])
    ident_d = din("c_ident", [128, 128], BF16)
    antij_d = din("c_antij", [128, 128], BF16)
    cmx_d = din("c_cmx", [128, 128], BF16)
    sele_d = din("c_sele", [128, 128], BF16)
    selo_d = din("c_selo", [128, 128], BF16)
    utri_d = din("c_utri", [128, 128])
    onesf_d = din("c_onesf", [128, 128])
    ohp_d = din("c_ohp", [33, 384])
    kaug_d = din("c_kaug", [2, 8, S], BF16)
    y_d = nc.dram_tensor("y", [nb, S, D], F32, kind="ExternalOutput").ap()
    dbg_d = nc.dram_tensor("dbg", [128, 16384], BF16, kind="ExternalOutput").ap() if stop else None
    wd_d = nc.dram_tensor("wd_scr", [8, 384], BF16, kind="Internal").ap()

    win_v = win_d.rearrange("(k p) n -> p k n", p=128)
    wout_v = wout_d.rearrange("(k p) n -> p k n", p=128)
    wup_v = wup_d.rearrange("(k p) n -> p k n", p=128)
    wdn_v = wdn_d.rearrange("(j p) n -> p j n", p=128)
    wgate_v = wgate_d.rearrange("(k p) n -> p k n", p=128)
    wproj_v = wproj_d.rearrange("(k p) n -> p k n", p=128)

    st = contextlib.ExitStack()
    with st:
        def sb(name, shape, dt):
            return st.enter_context(nc.sbuf_tensor(name, list(shape), dt))

        def ps(name, shape, dt):
            return st.enter_context(nc.psum_tensor(name, list(shape), dt))

        ident = sb("ident", [128, 128], BF16)
        antij = sb("antij", [128, 128], BF16)
        cmx = sb("cmx", [128, 128], BF16)
        sele = sb("sele", [128, 128], BF16)
        selo = sb("selo", [128, 128], BF16)
        utri = sb("utri", [128, 128], F32)
        onesf = sb("onesf", [128, 128], F32)
        ohp = sb("ohp", [33, 384], F32)
        relx = sb("relx", [33, 8], F32)
        wrow = sb("wrow", [8, 384], BF16)
        XT = sb("XT", [128, 8, 2, 128], BF16)
        c31b = sb("c31b", [128, 8], F32)
        biasM = sb("biasM", [128, 8], F32)
        gq4m = sb("gq4m", [128, 4, 64], F32)
        gq4f = sb("gq4f", [128, 4, 64], F32)
        gmx = sb("gmx", [128, 4], F32)
        negMm = sb("negMm", [128, 1], F32)
        negMf = sb("negMf", [128, 1], F32)
        bfb = sb("bfb", [128, 8], F32)
        gbuf = [sb("gbuf%d" % i, [128, D], F32) for i in range(2)]
        ssx = sb("ssx", [128, 1], F32)
        rsx = sb("rsx", [128, 1], F32)
        ssq4 = sb("ssq4", [128, 2, 4], F32)
        rs4 = sb("rs4", [128, 2, 4], F32)
        z8 = sb("z8", [128, 2, 8], F32)
        l8 = sb("l8", [128, 2, 8], F32)
        pref8 = sb("pref8", [128, 8], F32)
        ncb = sb("ncb", [128, NT, 8], F32)
        cqtok = sb("cqtok", [128, 128], BF16)
        msT = sb("msT", [16, S], BF16)
        cqT = msT
        mval = sb("mval", [128, 256], BF16)
        mval2 = [mval[:, 0:128], mval[:, 128:256]]
        gsb_t = sb("gsb", [128, 2, 2, 8], F32)
        gsb2 = [gsb_t[:, 0, :, :], gsb_t[:, 1, :, :]]
        top8_t = sb("top8", [128, 2, 2, 8], F32)
        top8b = [top8_t[:, 0, :, :], top8_t[:, 1, :, :]]
        km32 = sb("km32", [64, 2, 8], F32)
        kmT = sb("kmT", [128, 2, 8], BF16)
        Re = sb("Re", [128, 512], BF16)
        Ro = sb("Ro", [128, 512], BF16)
        bcs = sb("bcs", [128, 512], F32)
        den32 = sb("den32", [128, 512], F32)
        negones = sb("negones", [128, 1], F32)
        Pt = [sb("Pt%d" % i, [128, 512], BF16) for i in range(3)]
        hT = sb("hT", [128, 8 * S], BF16)
        OT = sb("OT", [128, 8 * S], BF16)
        slabs = [sb("slab%d" % i, [128, 8192], BF16) for i in range(NSLAB)]
        ar = sb("arena", [128, 25600], BF16)
        hTv = hT[:, :].rearrange("p (k s) -> p k s", k=8)
        uTv = hT[:, :].rearrange("p (j s) -> p j s", j=32)
        OTv = OT[:, :].rearrange("p (k s) -> p k s", k=8)
        xt = [OT[:, i * 2048:(i + 1) * 2048].bitcast(F32) for i in range(2)]
        hb = [OT[:, 4096 + i * 1024:4096 + (i + 1) * 1024] for i in range(2)]
        a1junk = OT[:, 6144:8192].bitcast(F32)
        QAs = [ar[:, o:o + 4096].rearrange("p (h s) -> p h s", h=2) for o in (0, 11264)]
        KAs = [ar[:, o + 4096:o + 8192].rearrange("p (h s) -> p h s", h=2) for o in (0, 11264)]
        VAs = [ar[:, o + 8192:o + 11264].rearrange("p (t c) -> p t c", c=192) for o in (0, 11264)]
        sqs2 = [ar[:, 22528 + i * 1536:23040 + i * 1536].bitcast(F32) for i in range(2)]
        tq2 = [ar[:, 23040 + i * 1536:23552 + i * 1536].bitcast(F32) for i in range(2)]
        qkn2 = [ar[:, 23552 + i * 1536:24064 + i * 1536].rearrange("p (a c) -> p a c", c=128) for i in range(2)]
        xc = [ar[:, i * 2048:(i + 1) * 2048].bitcast(F32) for i in range(4)]
        h2T = ar[:, 8192:12288].rearrange("p (k s) -> p k s", k=8)
        pT = ar[:, 12288:13312].rearrange("p (k s) -> p k s", k=2)
        r32 = [ar[:, 13312 + i * 1024:13312 + (i + 1) * 1024].bitcast(F32) for i in range(2)]
        sig = ar[:, 15360:17408].bitcast(F32)
        tmpf = ar[:, 17408:19456].bitcast(F32)
        hbC = Pt[0][:, :]
        hbC_t = sb("hbC", [128, 2 * D], BF16)
        hbC = hbC_t[:, 0:D]
        hbC2 = [hbC_t[:, 0:D], hbC_t[:, D:2 * D]]
        ssx2 = sb("ssx2", [128, 2], F32)
        rsx2 = sb("rsx2", [128, 2], F32)
        pf2_t = sb("pf2", [128, 512], F32)
        pb2_t = sb("pb2", [128, 512], BF16)
        pf2 = [pf2_t[:, 0:256], pf2_t[:, 256:512]]
        pb2 = [pb2_t[:, 0:256], pb2_t[:, 256:512]]
        pp01 = ps("pp01", [128, 1024], F32)
        tp = ps("tp", [128, 1024], BF16)
        Sps = [ps("S%d" % i, [128, 512], F32) for i in range(3)]
        pp67 = ps("pp67", [128, 1024], F32)
        ipp = [pp01[:, 0:512], pp01[:, 512:1024]]
        Ops = [pp67[:, 0:512], pp67[:, 512:1024]]
        BCv = tp[:, :].bitcast(F32)
        Gp2 = [BCv[:, 448:464], BCv[:, 464:480]]
        CUp = BCv[:, 480:488]
        tp8 = tp[:, :].rearrange("p (k s) -> p k s", k=8)

        P = Prog(nc)
        add = P.add
        lane_n = [0]

        def dma(q, out, in_, reads=(), writes=(), lane=None, name=""):
            if lane is None:
                lane = "u%d" % lane_n[0]
                lane_n[0] += 1
            return add(q, lambda e: [e.dma_start(out=out, in_=in_)], reads=reads, writes=writes, lane=lane, name=name)

        dma("q_sp", ident[:, :], ident_d, writes=["ident"])
        dma("q_sp", antij[:, :], antij_d, writes=["antij"])
        dma("q_sp", cmx[:, :], cmx_d, writes=["cmx"])
        dma("q_sp", sele[:, :], sele_d, writes=["sele"])
        dma("q_sp", selo[:, :], selo_d, writes=["selo"])
        dma("q_sp", utri[:, :], utri_d, writes=["utri"])
        dma("q_sp", onesf[:, :], onesf_d, writes=["onesf"])
        dma("q_sp", ohp[:, :], ohp_d, writes=["ohp"])
        dma("q_sp", relx[0:32, :], rel_d, writes=["relx"])
        add("pool", lambda e: e.memset(relx[32:33, :], -8192.0), writes=["relx32"])
        dma("q_sp", c31b[:, :], rel_d[31:32, :].partition_broadcast(128), writes=["c31b"])
        dma("q_sp", bfb[:, :], bf_d.partition_broadcast(128), writes=["bfb"])
        for i, (gt, srcs) in enumerate(((gq4m, (gqm_d, gqm_d, gkm_d, gkm_d)), (gq4f, (gqf_d, gqf_d, gkf_d, gkf_d)))):
            for j, sd in enumerate(srcs):
                dma("q_sp", gt[:, j, :], sd.partition_broadcast(128), writes=["g4_%d_%d" % (i, j)])
        add("pe", lambda e: e.matmul(Ops[0][0:8, 0:384], lhsT=relx[0:33, 0:8], rhs=ohp[0:33, 0:384], start=True, stop=True),
            reads=["relx", "relx32", "ohp"], writes=["O0"])
        add("act", lambda e: e.activation(out=wrow[:, :], in_=Ops[0][0:8, 0:384], func=AF.Copy, scale=8.0),
            reads=["O0"], writes=["wrow"])
        dma("q_act", wd_d, wrow[:, :], reads=["wrow"], writes=["wd"])
        for h in range(8):
            src = bass.AP(tensor=wd_d.tensor, offset=wd_d.offset + h * 384, ap=[[1, 128], [128, 2], [1, 128]])
            dma("q_act", XT[:, h, :, :], src, reads=["wd"], writes=["XT"])
        for i, gt in enumerate((gq4m, gq4f)):
            nm = (negMm, negMf)[i]
            add("dve", lambda e, gt=gt: e.tensor_reduce(out=gmx[:, 0:4], in_=gt[:, :, :], axis=AX.X, op=ALU.max,
                                                         apply_absolute_value=True),
                reads=["g4_%d_%d" % (i, j) for j in range(4)], writes=["gmx"])
            add("dve", lambda e, nm=nm: e.scalar_tensor_tensor(out=nm[:, :], in0=gmx[:, 0:1], scalar=-8.0, in1=gmx[:, 2:3],
                                                                op0=ALU.mult, op1=ALU.mult),
                reads=["gmx"], writes=["negM%d" % i])
        add("dve", lambda e: e.tensor_scalar(out=biasM[:, :], in0=c31b[:, :], scalar1=negMm[:, 0:1], scalar2=0.0, op0=ALU.add, op1=ALU.add),
            reads=["c31b", "negM0"], writes=["biasM"])
        add("pool", lambda e: e.memset(negones[:], -1.0), writes=["negones"])
        for t_, nm_ in ((Re, "Re"), (Ro, "Ro"), (mval, "mval"), (cqtok, "cqtok"), (kmT, "kmT")):
            add("pool", lambda e, t_=t_: e.memset(t_[:], 0.0), writes=[nm_])

        slab_seq = []
        slab_issued = [0]
        slab_base = [0]

        def slab_issue(k):
            kind, arg = slab_seq[k]
            slot = k % NSLAB
            sl = slabs[slot]
            key = "slab%d" % slot
            lane = "slab%d" % slot
            if kind == "win":
                g = arg
                base = (0 if g < 4 else 1536) + 128 * (g % 4)
                v = sl[:, 0:8 * 392].rearrange("p (k n) -> p k n", n=392)
                pieces = [(v[:, :, 0:128], win_v[:, :, base:base + 128]),
                          (v[:, :, 128:256], win_v[:, :, base + 512:base + 640]),
                          (v[:, :, 256:384], win_v[:, :, base + 1024:base + 1152])]
                if g == 4:
                    pieces.append((v[:, :, 384:392], win_v[:, :, 3072:3080]))
            elif kind == "wout":
                v = sl[:, :].rearrange("p (k n) -> p k n", n=1024)
                pieces = [(v[:, 0:4, :], wout_v[:, 0:4, :]), (v[:, 4:8, :], wout_v[:, 4:8, :])]
            elif kind == "wgate":
                v = sl[:, :].rearrange("p (k n) -> p k n", n=1024)
                pieces = [(v[:, 0:4, :], wgate_v[:, 0:4, :]), (v[:, 4:8, :], wgate_v[:, 4:8, :])]
            elif kind == "wup":
                v = sl[:, :].rearrange("p (k n) -> p k n", n=1024)
                c0 = arg * 1024
                pieces = [(v[:, 0:4, :], wup_v[:, 0:4, c0:c0 + 1024]), (v[:, 4:8, :], wup_v[:, 4:8, c0:c0 + 1024])]
            elif kind == "wdn":
                v = sl[:, :].rearrange("p (j n) -> p j n", n=256)
                c0 = arg * 256
                pieces = [(v[:, 8 * q:8 * q + 8, :], wdn_v[:, 8 * q:8 * q + 8, c0:c0 + 256]) for q in range(4)]
            elif kind == "wproj":
                v = sl[:, 0:2048].rearrange("p (k n) -> p k n", n=1024)
                pieces = [(v[:, :, :], wproj_v[:, :, :])]
            add("q_pool", lambda e, pieces=pieces: [e.dma_start(out=o, in_=i) for o, i in pieces],
                writes=[key], lane=lane, ndma=len(pieces), name="slab%d" % k)

        def use_slab(k, ahead=2):
            while slab_issued[0] <= min(k + ahead, len(slab_seq) - 1):
                slab_issue(slab_issued[0])
                slab_issued[0] += 1
            return k % NSLAB

        for b in range(nb):
            for g in range(8):
                slab_seq.append(("win", g))
            for tb in range(4):
                slab_seq.append(("wout", 0))
                for jg in range(4):
                    slab_seq.append(("wup", jg))
                for nq in range(4):
                    slab_seq.append(("wdn", nq))
                slab_seq.append(("wgate", 0))
                slab_seq.append(("wproj", 0))
        slab_ptr = [0]

        def next_slab(ahead=2):
            k = slab_ptr[0]
            slab_ptr[0] += 1
            return use_slab(k, ahead)

        def rmsnorm_to_T(xtile, xkey, gtile, gkey, dstT, dstkey, col0, junk, junkkey):
            add("act", lambda e: e.activation(out=junk, in_=xtile, func=AF.Square, accum_out=ssx[:, :]),
                reads=[xkey], writes=[junkkey, "ssx"])
            add("act", lambda e: e.activation(out=rsx[:, :], in_=ssx[:, :], func=AF.Ln, bias=EPS, scale=1.0 / D),
                reads=["ssx"], writes=["rsx"])
            add("act", lambda e: e.activation(out=rsx[:, :], in_=rsx[:, :], func=AF.Exp, scale=-0.5),
                reads=["rsx"], writes=["rsx"])
            return None

        def norm_apply(xtile, xkey, gtile, gkey, hbt, hbkey):
            add("dve", lambda e: e.scalar_tensor_tensor(out=hbt, in0=xtile, scalar=rsx[:, 0:1], in1=gtile,
                                                        op0=ALU.mult, op1=ALU.mult),
                reads=[xkey, "rsx", gkey], writes=[hbkey])

        def transpose8(hbt, hbkey, dst, dstkey, eng="act"):
            def f(e):
                r = None
                for k in range(8):
                    r = e.transpose(tp8[:, k, :], hbt[:, k * 128:(k + 1) * 128], ident[:, :])
                return r
            add("pe", f, reads=[hbkey, "ident"], writes=["tp"])
            if eng == "act":
                add("act", lambda e: e.activation(out=dst, in_=tp8[:, :, :], func=AF.Copy), reads=["tp"], writes=[dstkey])
            else:
                add("dve", lambda e: e.tensor_copy(out=dst, in_=tp8[:, :, :]), reads=["tp"], writes=[dstkey])

        def check(tag):
            if stop == tag:
                raise _Stop()

        try:
            check('setup')
            for b in range(nb):
                for VA_ in VAs:
                    add("pool", lambda e, VA_=VA_: e.memset(VA_[:, :, 64:128], 0.0), writes=["VAc"])
                    add("pool", lambda e, VA_=VA_: e.memset(VA_[:, :, 64:65], 1.0), reads=["VAc"], writes=["VAc"])
                add("pool", lambda e: e.memset(qkn2[0][:, :, :], 0.0), writes=["qkn0"])
                add("pool", lambda e: e.memset(qkn2[1][:, :, :], 0.0), writes=["qkn1"])
                pending_norm = []
                qbi = [0]
                sc = [0]

                def flush_norm():
                    while pending_norm:
                        pending_norm.pop(0)()
                add("pool", lambda e: e.memset(ar[64:128, 0:8192], 0.0), writes=[("QAaug", 0, 0), ("QAaug", 0, 1), ("KAaug", 0)])
                add("pool", lambda e: e.memset(ar[64:128, 11264:11264 + 8192], 0.0), writes=[("QAaug", 1, 0), ("QAaug", 1, 1), ("KAaug", 1)])

                dma("q_sp", gbuf[0][:, :], gattn_d.partition_broadcast(128), writes=["gbuf0"], lane="gb0")
                for tt in range(NT):
                    i2 = tt % 2
                    dma("q_sp", xt[i2], x_d[b, tt * 128:(tt + 1) * 128, :], writes=["xt%d" % i2], lane="xt%d" % i2)
                    rmsnorm_to_T(xt[i2], "xt%d" % i2, None, None, None, None, 0, a1junk, "a1junk")
                    norm_apply(xt[i2], "xt%d" % i2, gbuf[0][:, :], "gbuf0", hb[i2], "hb%d" % i2)
                    transpose8(hb[i2], "hb%d" % i2, hTv[:, :, tt * 128:(tt + 1) * 128], ("hT", tt), eng=("act" if tt % 2 else "dve"))
                def inproj_steps(g):
                    steps = []
                    moba = g < 4
                    st_ = g % 2
                    QA, KA, VA = QAs[st_], KAs[st_], VAs[st_]
                    gain4 = gq4m if moba else gq4f
                    gkeys = ["g4_%d_%d" % (0 if moba else 1, j) for j in range(4)]
                    NW = 392 if g == 4 else 384
                    ctx = {}

                    def prologue():
                        slot = next_slab()
                        ctx["skey"] = "slab%d" % slot
                        ctx["wv"] = slabs[slot][:, 0:8 * 392].rearrange("p (k n) -> p k n", n=392)
                        if g in (0, 1, 4, 5):
                            ty = 0 if moba else 1
                            add("q_sp", lambda e: [e.dma_start(out=KA[64:72, 0, :], in_=kaug_d[ty]),
                                                   e.dma_start(out=KA[64:72, 1, :], in_=kaug_d[ty])],
                                writes=[("KAaug", st_)], lane="kaug%d" % st_, ndma=2)
                        if moba:
                            add("pool", lambda e: e.memset(QA[64:72, :, :], 0.0), writes=[("QAaug", st_, 0), ("QAaug", st_, 1)])
                        stage0(0)

                    def stage0(tt):
                        ip = ipp[tt % 2]
                        ipk = "ipp%d" % (tt % 2)
                        cs = slice(tt * 128, (tt + 1) * 128)
                        wv = ctx["wv"]
                        skey = ctx["skey"]

                        def f(e):
                            r = None
                            for k in range(8):
                                r = e.matmul(ip[:, 0:NW], lhsT=hTv[:, k, cs], rhs=wv[:, k, 0:NW], start=(k == 0), stop=(k == 7))
                            return r
                        add("pe", f, reads=[("hT", tt), skey], writes=[ipk])

                    def stage1a(tt):
                        ip = ipp[tt % 2]
                        ipk = "ipp%d" % (tt % 2)
                        d2 = tt % 2
                        add("act", lambda e: e.activation(out=sqs2[d2], in_=ip[:, 0:256], func=AF.Square), reads=[ipk], writes=["sqs%d" % d2])
                        add("dve", lambda e: e.tensor_reduce(out=ssq4[:, d2, :], in_=sqs2[d2].rearrange("p (a c) -> p a c", c=64), axis=AX.X, op=ALU.add),
                            reads=["sqs%d" % d2], writes=["ssq4_%d" % d2])
                        add("dve", lambda e: e.tensor_tensor(out=tq2[d2], in0=ip[:, 0:256],
                                                             in1=gain4[:, :, :].rearrange("p a c -> p (a c)"), op=ALU.mult),
                            reads=[ipk] + gkeys, writes=["tq%d" % d2])
                        add("dve", lambda e: e.tensor_copy(out=VA[:, tt, 0:64], in_=ip[:, 256:320]),
                            reads=[ipk], writes=[("VA", st_, tt, 0)])
                        add("dve", lambda e: e.tensor_copy(out=VA[:, tt, 128:192], in_=ip[:, 320:384]),
                            reads=[ipk], writes=[("VA", st_, tt, 1)])
                        if g == 4:
                            add("dve", lambda e: e.tensor_tensor(out=z8[:, d2, :], in0=ip[:, 384:392], in1=bfb[:, :], op=ALU.add),
                                reads=[ipk, "bfb"], writes=["z8_%d" % d2])

                    def stage1b(tt):
                        d2 = tt % 2
                        add("act", lambda e: e.activation(out=rs4[:, d2, :], in_=ssq4[:, d2, :], func=AF.Ln, bias=EPS, scale=1.0 / 64),
                            reads=["ssq4_%d" % d2], writes=["rs4_%d" % d2])
                        add("act", lambda e: e.activation(out=rs4[:, d2, :], in_=rs4[:, d2, :], func=AF.Exp, scale=-0.5),
                            reads=["rs4_%d" % d2], writes=["rs4_%d" % d2])
                        add("pool", lambda e: e.tensor_tensor(out=qkn2[d2][:, :, 0:64], in0=tq2[d2].rearrange("p (a c) -> p a c", c=64),
                                                              in1=rs4[:, d2, :].unsqueeze(2).to_broadcast([128, 4, 64]), op=ALU.mult),
                            reads=["tq%d" % d2, "rs4_%d" % d2], writes=["qkn%d" % d2])
                        if g == 4:
                            add("act", lambda e: e.activation(out=z8[:, d2, :], in_=z8[:, d2, :], func=AF.Exp, scale=-1.0),
                                reads=["z8_%d" % d2], writes=["z8_%d" % d2])
                            add("act", lambda e: e.activation(out=l8[:, d2, :], in_=z8[:, d2, :], func=AF.Ln, bias=1.0, scale=1.0),
                                reads=["z8_%d" % d2], writes=["l8_%d" % d2])

                    def stage2_pe(tt):
                        d2 = tt % 2

                        def ft(e):
                            r = None
                            for a in range(4):
                                r = e.transpose(tp8[:, a, :], qkn2[d2][:, a, :], ident[:, :])
                            return r
                        add("pe", ft, reads=["qkn%d" % d2, "ident"], writes=["tp"])

                    def stage2_rest(tt):
                        d2 = tt % 2
                        cs = slice(tt * 128, (tt + 1) * 128)
                        add("dve", lambda e: e.tensor_copy(out=QA[0:64, :, cs], in_=tp8[0:64, 0:2, :]),
                            reads=["tp"], writes=[("QA", st_, tt)])
                        add("dve", lambda e: e.tensor_copy(out=KA[0:64, :, cs], in_=tp8[0:64, 2:4, :]),
                            reads=["tp"], writes=[("KA", st_, tt)])
                        if g == 4:
                            def fc(e):
                                r = e.matmul(CUp, lhsT=utri[:, :], rhs=l8[:, d2, :], start=True, stop=(tt == 0))
                                if tt > 0:
                                    r = e.matmul(CUp, lhsT=onesf[:, :], rhs=pref8[:, :], start=False, stop=True)
                                return r
                            add("pe", fc, reads=["l8_%d" % d2, "pref8", "utri", "onesf"], writes=["tp"])
                            add("dve", lambda e: e.tensor_scalar(out=ncb[:, tt, :], in0=CUp, scalar1=negMf[:, 0:1], scalar2=0.0, op0=ALU.add, op1=ALU.add),
                                reads=["tp", "negM1"], writes=[("ncb", tt)])
                            add("dve", lambda e: e.tensor_scalar(out=cqtok[:, 0:8], in0=CUp, scalar1=-8.0, scalar2=1.0, op0=ALU.mult, op1=ALU.mult),
                                reads=["tp", "cqtok"], writes=["cqtok"])
                            if tt == 0:
                                add("dve", lambda e: e.tensor_copy(out=pref8[:, :], in_=l8[:, d2, :]), reads=["l8_%d" % d2], writes=["pref8"])
                            else:
                                add("dve", lambda e: e.tensor_tensor(out=pref8[:, :], in0=pref8[:, :], in1=l8[:, d2, :], op=ALU.add),
                                    reads=["l8_%d" % d2, "pref8"], writes=["pref8"])
                            add("pe", lambda e: e.transpose(tp8[:, 4, :], cqtok[:, :], ident[:, :]), reads=["cqtok", "ident"], writes=["tp"])
                            add("act", lambda e: e.activation(out=cqT[0:8, cs], in_=tp8[0:8, 4, :], func=AF.Copy),
                                reads=["tp"], writes=["msT"])

                    steps.append(prologue)
                    for n_ in range(1, NT + 3):
                        def it(n_=n_):
                            if IPIPE == 4:
                                if 0 <= n_ - 3 < NT:
                                    stage2_pe(n_ - 3)
                                if n_ < NT:
                                    stage0(n_)
                                if 0 <= n_ - 1 < NT:
                                    stage1a(n_ - 1)
                                if 0 <= n_ - 2 < NT:
                                    stage1b(n_ - 2)
                                if 0 <= n_ - 3 < NT:
                                    stage2_rest(n_ - 3)
                            else:
                                t_ = n_ - 1
                                if t_ == 0:
                                    stage1a(0)
                                    stage1b(0)
                                if t_ < NT:
                                    if t_ + 1 < NT:
                                        stage0(t_ + 1)
                                        stage1a(t_ + 1)
                                        stage1b(t_ + 1)
                                    stage2_pe(t_)
                                    stage2_rest(t_)
                        steps.append(it)
                    if moba:
                        def kmstep():
                            for hl in range(2):
                                add("dve", lambda e, hl=hl: e.tensor_reduce(out=km32[:, hl, :], in_=KA[0:64, hl, :].rearrange("p (n c) -> p n c", c=256),
                                                                            axis=AX.X, op=ALU.add),
                                    reads=[("KA", st_, t_) for t_ in range(NT)], writes=["km32"])
                            add("dve", lambda e: e.tensor_scalar(out=kmT[0:64, :, :], in0=km32[:, :, :], scalar1=1.0 / 256, scalar2=1.0, op0=ALU.mult, op1=ALU.mult),
                                reads=["km32", "kmT"], writes=["kmT"])
                        steps.append(kmstep)
                        def m_a(tt):
                            own = tt // 2
                            m2 = tt % 2
                            cs = slice(tt * 128, (tt + 1) * 128)
                            Gv = Gp2[m2]

                            def fg(e):
                                r = None
                                for hl in range(2):
                                    r = e.matmul(Gv[:, hl * 8:hl * 8 + 8], lhsT=QA[:, hl, cs], rhs=kmT[:, hl, :], start=True, stop=True)
                                return r
                            add("pe", fg, reads=[("QA", st_, tt), ("QAaug", st_, 0), ("QAaug", st_, 1), "kmT"], writes=["tp"])
                            add("pool", lambda e: e.memset(gsb2[m2][:, :, :], -1e30), writes=[("gsb", m2)])
                            add("pool", lambda e: e.memset(mval2[m2][:, 0:16], 0.0), writes=[("mval", m2)])
                            add("dve", lambda e: e.tensor_copy(out=gsb2[m2][:, :, 0:own],
                                                               in_=Gv.rearrange("p (h n) -> p h n", n=8)[:, :, 0:own]),
                                reads=["tp", ("gsb", m2)], writes=[("gsb", m2)])
                            for hl in range(2):
                                add("dve", lambda e, hl=hl: e.max(out=top8b[m2][:, hl, :], in_=gsb2[m2][:, hl, :]), reads=[("gsb", m2)], writes=[("top8", m2, hl)])
                                add("dve", lambda e, hl=hl: e.tensor_scalar(out=mval2[m2][:, hl * 8:hl * 8 + own], in0=gsb2[m2][:, hl, 0:own],
                                                                            scalar1=top8b[m2][:, hl, 2:3], scalar2=1.0, op0=ALU.is_lt, op1=ALU.mult),
                                    reads=[("gsb", m2), ("top8", m2, hl), ("mval", m2)], writes=[("mval", m2)])

                        def m_b(tt):
                            m2 = tt % 2
                            cs = slice(tt * 128, (tt + 1) * 128)
                            add("pe", lambda e: e.transpose(tp8[:, 5, :], mval2[m2][:, :], ident[:, :]), reads=[("mval", m2), "ident"], writes=["tp"])
                            add("act", lambda e: e.activation(out=msT[0:16, cs], in_=tp8[0:16, 5, :], func=AF.Copy),
                                reads=["tp"], writes=["msT"])

                        for tt in range(8, NT + 1):
                            def mstep(tt=tt):
                                if tt - 1 >= 8:
                                    m_b(tt - 1)
                                if tt < NT:
                                    m_a(tt)
                            steps.append(mstep)

                        def augdma():
                            for hl in range(2):
                                dma("q_sp", QA[64:72, hl, 1024:2048], msT[hl * 8:hl * 8 + 8, 1024:2048], reads=["msT"],
                                    writes=[("QAaug", st_, hl)], lane="qaug%d_%d" % (st_, hl))
                        steps.append(augdma)
                    else:
                        def augdma():
                            for hl in range(2):
                                hf = 2 * (g - 4) + hl
                                dma("q_sp", QA[64:65, hl, :], cqT[hf:hf + 1, :], reads=["msT"], writes=[("QAaug", st_, hl)],
                                    lane="qaug%d_%d" % (st_, hl))
                        steps.append(augdma)
                    return steps

                def attn_steps(g):
                    moba = g < 4
                    st_ = g % 2
                    QA, KA, VA = QAs[st_], KAs[st_], VAs[st_]
                    ents = []
                    for hl in range(2):
                        h = 2 * g + hl
                        hm = h if moba else h - 8
                        for qb in range(4):
                            nkt = 4 * (qb + 1)
                            ob = qbi[0] % 2
                            qbi[0] += 1
                            for kt in range(nkt):
                                j = kt - 4 * qb
                                c0 = 128 * j if j >= 0 else 0
                                extras = []
                                if moba:
                                    if j >= 0:
                                        extras.append((c0, XT[:, hm, 1, :]))
                                        if j < 3:
                                            extras.append((c0 + 128, XT[:, hm, 0, :]))
                                    elif j == -1:
                                        extras.append((0, XT[:, hm, 0, :]))
                                else:
                                    if j >= 0:
                                        extras.append((c0, cmx[:, :]))
                                bi = sc[0] % 3
                                sc[0] += 1
                                ents.append(dict(hl=hl, hm=hm, qb=qb, kt=kt, nkt=nkt, q0=qb * 512, c0=c0, extras=extras, bi=bi,
                                                 Ob=Ops[ob], okey="O%d" % ob))
                    n = len(ents)

                    def do_S(i):
                        en = ents[i]
                        bi, c0, extras, hl, q0, kt, qb = en["bi"], en["c0"], en["extras"], en["hl"], en["q0"], en["kt"], en["qb"]
                        ks = slice(kt * 128, (kt + 1) * 128)

                        def fs(e):
                            r = e.matmul(Sps[bi][:, c0:512], lhsT=KA[:, hl, ks], rhs=QA[:, hl, q0 + c0:q0 + 512],
                                         start=True, stop=(len(extras) == 0))
                            for n_, (cc, lt) in enumerate(extras):
                                r = e.matmul(Sps[bi][:, cc:cc + 128], lhsT=lt, rhs=antij[:, :], start=False,
                                             stop=(n_ == len(extras) - 1))
                            return r
                        add("pe", fs, reads=[("KA", st_, kt), ("KAaug", st_), ("QAaug", st_, hl), "XT", "cmx", "antij"]
                            + [("QA", st_, 4 * qb + t_) for t_ in range(4)], writes=["S%d" % bi])

                    def do_E(i):
                        en = ents[i]
                        bi, c0, hm, kt = en["bi"], en["c0"], en["hm"], en["kt"]
                        if moba:
                            bias_ap = biasM[:, hm:hm + 1]
                            bkeys = ["biasM"]
                        else:
                            bias_ap = ncb[:, kt, hm:hm + 1]
                            bkeys = [("ncb", kt)]
                        add("act", lambda e: e.activation(out=Pt[bi][:, c0:512], in_=Sps[bi][:, c0:512],
                                                          func=AF.Exp, bias=bias_ap, scale=SCALE),
                            reads=["S%d" % bi] + bkeys, writes=["P%d" % bi])

                    def do_PV(i):
                        en = ents[i]
                        bi, c0, hl, kt, nkt, Ob, okey, qb, q0 = (en["bi"], en["c0"], en["hl"], en["kt"], en["nkt"], en["Ob"], en["okey"],
                                                                 en["qb"], en["q0"])

                        def fpv(e):
                            if hl == 0:
                                return e.matmul(Ob[:, c0:512], lhsT=VA[:, kt, 0:128], rhs=Pt[bi][:, c0:512],
                                                start=(kt == 0), stop=(kt == nkt - 1))
                            return e.matmul(Ob[:, c0:512], lhsT=VA[:, kt, 64:192], rhs=Pt[bi][:, c0:512],
                                            start=(kt == 0), stop=(kt == nkt - 1))
                        add("pe", fpv, reads=["P%d" % bi, ("VA", st_, kt, hl), "VAc"], writes=[okey])
                        if kt == nkt - 1:
                            ocs = slice(q0, q0 + 512)
                            if hl == 0:
                                Rt, rk, row, sel, selk, rows = Re, "Re", slice(64, 65), sele, "sele", slice(0, 64)
                            else:
                                Rt, rk, row, sel, selk, rows = Ro, "Ro", slice(0, 1), selo, "selo", slice(64, 128)
                            add("act", lambda e: e.activation(out=den32[row, :], in_=Ob[row, :], func=AF.Ln), reads=[okey], writes=["den32"])
                            add("act", lambda e: e.activation(out=Rt[row, :], in_=den32[row, :], func=AF.Exp, scale=-1.0),
                                reads=["den32", rk], writes=[rk])

                            def norm():
                                add("pe", lambda e: e.matmul(BCv, lhsT=sel[:, :], rhs=Rt[:, :], start=True, stop=True),
                                    reads=[rk, selk], writes=["tp"])
                                add("dve", lambda e: e.tensor_copy(out=bcs[rows, :], in_=BCv[rows, :]), reads=["tp"], writes=["bcs"])
                                add("dve", lambda e: e.tensor_tensor(out=OTv[rows, g, ocs], in0=Ob[rows, :], in1=bcs[rows, :], op=ALU.mult),
                                    reads=[okey, "bcs"], writes=[("OT", g, qb, hl)] + OT_ALIAS.get(g, []))
                            pending_norm.append(norm)

                    steps = []
                    for i in range(n):
                        def step(i=i):
                            if i == 0:
                                do_S(0)
                                if n > 1:
                                    do_S(1)
                            do_E(i)
                            if i > 0:
                                do_PV(i - 1)
                            if i + 2 < n:
                                do_S(i + 2)
                            if ents[i]["kt"] == 3:
                                flush_norm()
                            if i == n - 1:
                                do_PV(i)
                                flush_norm()
                        steps.append(step)
                    return steps

                for f_ in inproj_steps(0):
                    f_()
                check('ip0')
                for g in range(8):
                    A_ = attn_steps(g)
                    I_ = inproj_steps(g + 1) if g < 7 else []
                    ni = 0
                    for ia, a_ in enumerate(A_):
                        a_()
                        tgt = ((ia + 1) * len(I_)) // len(A_)
                        while ni < tgt:
                            I_[ni]()
                            ni += 1
                    while ni < len(I_):
                        I_[ni]()
                        ni += 1
                    flush_norm()
                    check('g%d' % g)
                    check('attn%d' % g)
                P.barrier()
                check('B')

                dma("q_sp", gbuf[0][:, :], gmlp_d.partition_broadcast(128), writes=["gbuf0"], lane="gb0")
                dma("q_sp", gbuf[1][:, :], gple_d.partition_broadcast(128), writes=["gbuf1"], lane="gb1")
                junkC = ar[:, 13312:15360].bitcast(F32)
                tpS2 = Sps[2][:, :].bitcast(BF16).rearrange("p (k s) -> p k s", k=8)

                def c_norm(i, par, gt, gk):
                    xk = "xc%d" % i
                    add("act", lambda e: e.activation(out=junkC, in_=xc[i], func=AF.Square, accum_out=ssx2[:, par:par + 1]),
                        reads=[xk], writes=["r32_0", "r32_1", ("ssx", par)])
                    add("act", lambda e: e.activation(out=rsx2[:, par:par + 1], in_=ssx2[:, par:par + 1], func=AF.Ln, bias=EPS, scale=1.0 / D),
                        reads=[("ssx", par)], writes=[("rsx", par)])
                    add("act", lambda e: e.activation(out=rsx2[:, par:par + 1], in_=rsx2[:, par:par + 1], func=AF.Exp, scale=-0.5),
                        reads=[("rsx", par)], writes=[("rsx", par)])
                    add("dve", lambda e: e.scalar_tensor_tensor(out=hbC2[par], in0=xc[i], scalar=rsx2[:, par:par + 1], in1=gt,
                                                                op0=ALU.mult, op1=ALU.mult),
                        reads=[xk, ("rsx", par), gk], writes=[("hbC", par)])

                def c_tr(i, par):
                    def f(e):
                        r = None
                        for k in range(8):
                            r = e.transpose(tp8[:, k, :], hbC2[par][:, k * 128:(k + 1) * 128], ident[:, :])
                        return r
                    add("pe", f, reads=[("hbC", par), "ident"], writes=["tp"])
                    add("act", lambda e: e.activation(out=h2T[:, :, i * 128:(i + 1) * 128], in_=tp8[:, :, :], func=AF.Copy),
                        reads=["tp"], writes=[("h2T", i)])

                for tb in range(4):
                    s_out = next_slab()
                    wo = slabs[s_out][:, :].rearrange("p (k n) -> p k n", n=1024)
                    PPo = [(pp01, ["pp01", "pp01h0", "pp01h1"]), (pp67, ["pp67"])]

                    def op_s0(i, tb=tb, wo=wo, s_out=s_out):
                        tt = 4 * tb + i
                        cs = slice(tt * 128, (tt + 1) * 128)
                        pp_, pk_ = PPo[i % 2]
                        dma("q_sp", xc[i], x_d[b, cs, :], writes=["xc%d" % i], lane="xc%d" % i)

                        def fo(e):
                            r = None
                            for n in range(2):
                                for k in range(8):
                                    r = e.matmul(pp_[:, n * 512:(n + 1) * 512], lhsT=OTv[:, k, cs], rhs=wo[:, k, n * 512:(n + 1) * 512],
                                                 start=(k == 0), stop=(k == 7))
                            return r
                        add("pe", fo, reads=["slab%d" % s_out] + [("OT", k, tt // 4, h_) for k in range(8) for h_ in range(2)], writes=pk_)

                    def op_s1(i):
                        pp_, pk_ = PPo[i % 2]
                        xk = "xc%d" % i
                        add("dve", lambda e: e.tensor_tensor(out=xc[i], in0=pp_[:, :], in1=xc[i], op=ALU.add), reads=pk_ + [xk], writes=[xk])
                        c_norm(i, i % 2, gbuf[0][:, :], "gbuf0")

                    op_s0(0)
                    for i in range(4):
                        if i + 1 < 4:
                            op_s0(i + 1)
                        op_s1(i)
                        c_tr(i, i % 2)
                    for jg in range(4):
                        s_up = next_slab()
                        wu = slabs[s_up][:, :].rearrange("p (k n) -> p k n", n=1024)
                        for jj in range(8):
                            j = 8 * jg + jj
                            bi = j % 3
                            ri = j % 2

                            def fu(e, bi=bi, jj=jj, wu=wu):
                                r = None
                                for k in range(8):
                                    r = e.matmul(Sps[bi][:, :], lhsT=wu[:, k, jj * 128:(jj + 1) * 128], rhs=h2T[:, k, :], start=(k == 0), stop=(k == 7))
                                return r
                            add("pe", fu, reads=["slab%d" % s_up] + [("h2T", i) for i in range(4)], writes=["S%d" % bi])
                            add("act", lambda e, bi=bi, ri=ri: e.activation(out=r32[ri], in_=Sps[bi][:, :], func=AF.Relu), reads=["S%d" % bi], writes=["r32_%d" % ri])
                            add("dve", lambda e, ri=ri, j=j: e.tensor_tensor(out=uTv[:, j, :], in0=r32[ri], in1=r32[ri], op=ALU.mult),
                                reads=["r32_%d" % ri], writes=[("uT", j)])
                    cnt = 0
                    for nq in range(4):
                        s_dn = next_slab()
                        wdv = slabs[s_dn][:, :].rearrange("p (j n) -> p j n", n=256)
                        for i in range(4):
                            hb_ = cnt % 2
                            cnt += 1
                            xk = "xc%d" % i

                            def fd(e, hb_=hb_, i=i, wdv=wdv):
                                r = None
                                for j in range(32):
                                    r = e.matmul(pp01[:, hb_ * 512:hb_ * 512 + 256], lhsT=uTv[:, j, i * 128:(i + 1) * 128], rhs=wdv[:, j, :],
                                                 start=(j == 0), stop=(j == 31))
                                return r
                            add("pe", fd, reads=["slab%d" % s_dn] + [("uT", j) for j in range(32)], writes=["pp01h%d" % hb_])
                            add("dve", lambda e, hb_=hb_, i=i, nq=nq: e.tensor_tensor(out=xc[i][:, nq * 256:(nq + 1) * 256],
                                                                                      in0=pp01[:, hb_ * 512:hb_ * 512 + 256],
                                                                                      in1=xc[i][:, nq * 256:(nq + 1) * 256], op=ALU.add),
                                reads=["pp01h%d" % hb_, xk], writes=[xk])
                    s_g = next_slab()
                    s_p = next_slab(ahead=1)
                    wg = slabs[s_g][:, :].rearrange("p (k n) -> p k n", n=1024)
                    wp = slabs[s_p][:, 0:2048].rearrange("p (k n) -> p k n", n=1024)
                    GO = [([pp01[:, 0:512], pp01[:, 512:1024]], ["pp01", "pp01h0", "pp01h1"]), ([Sps[0][:, :], Sps[1][:, :]], ["S0", "S1"])]

                    def ple_a(i, tb=tb):
                        tt = 4 * tb + i
                        cs = slice(tt * 128, (tt + 1) * 128)
                        par = i % 2
                        c_norm(i, par, gbuf[1][:, :], "gbuf1")
                        dma("q_sp", pf2[par], p_d[b, cs, :], writes=[("pf", par)], lane="pf%d" % par)
                        add("dve", lambda e: e.tensor_copy(out=pb2[par], in_=pf2[par]), reads=[("pf", par)], writes=[("pb", par)])

                    def ple_b(i):
                        par = i % 2
                        c_tr(i, par)

                        def fpt(e):
                            e.transpose(tpS2[:, 0, :], pb2[par][:, 0:128], ident[:, :])
                            return e.transpose(tpS2[:, 1, :], pb2[par][:, 128:256], ident[:, :])
                        add("pe", fpt, reads=[("pb", par), "ident"], writes=["S2"])
                        add("dve", lambda e: e.tensor_copy(out=pT[:, :, i * 128:(i + 1) * 128], in_=tpS2[:, 0:2, :]), reads=["S2"], writes=[("pT", i)])

                    def ple_c(i, wg=wg, wp=wp, s_g=s_g, s_p=s_p):
                        gb_, gk_ = GO[i % 2]

                        def fg2(e):
                            r = None
                            for n in range(2):
                                for k in range(8):
                                    r = e.matmul(gb_[n], lhsT=h2T[:, k, i * 128:(i + 1) * 128], rhs=wg[:, k, n * 512:(n + 1) * 512],
                                                 start=(k == 0), stop=(k == 7))
                            return r
                        add("pe", fg2, reads=["slab%d" % s_g, ("h2T", i)], writes=gk_)

                        def fp2(e):
                            r = None
                            for n in range(2):
                                for k in range(2):
                                    r = e.matmul(pp67[:, n * 512:(n + 1) * 512], lhsT=pT[:, k, i * 128:(i + 1) * 128], rhs=wp[:, k, n * 512:(n + 1) * 512],
                                                 start=(k == 0), stop=(k == 1))
                            return r
                        add("pe", fp2, reads=["slab%d" % s_p, ("pT", i)], writes=["pp67"])

                    def ple_d(i, tb=tb):
                        tt = 4 * tb + i
                        cs = slice(tt * 128, (tt + 1) * 128)
                        xk = "xc%d" % i
                        gb_, gk_ = GO[i % 2]
                        for n in range(2):
                            add("act", lambda e, n=n: e.activation(out=sig[:, n * 512:(n + 1) * 512], in_=gb_[n], func=AF.Sigmoid),
                                reads=gk_, writes=["sig%d" % n])
                        add("dve", lambda e: e.tensor_tensor(out=tmpf, in0=pp67[:, :], in1=sig, op=ALU.mult), reads=["pp67", "sig0", "sig1", "tmpf"], writes=["tmpf"])
                        add("dve", lambda e: e.tensor_tensor(out=xc[i], in0=xc[i], in1=tmpf, op=ALU.add), reads=["tmpf", xk], writes=[xk])
                        dma("q_sp", y_d[b, cs, :], xc[i], reads=[xk], lane="out%d" % i)

                    for fn_, i_ in ((ple_a, 0), (ple_a, 1), (ple_b, 0), (ple_c, 0), (ple_b, 1), (ple_d, 0), (ple_c, 1), (ple_a, 2),
                                    (ple_b, 2), (ple_d, 1), (ple_c, 2), (ple_a, 3), (ple_b, 3), (ple_d, 2), (ple_c, 3), (ple_d, 3)):
                        fn_(i_)
                P.barrier()


        except _Stop:
            P.barrier()
            src = dbg(locals()) if dbg is not None else ident[:, :]
            dma('q_sp', dbg_d[0:src.shape[0], 0:src.shape[1]], src, lane='dbgout')
        P.finalize(st)
        fin = [(("lane", l), P.lane_cnt[l]) for l in ["out%d" % i for i in range(4)] + ["dbgout"] if l in P.lane_cnt]
        with nc.allow_low_precision(reason="bf16 matmul operands, fp32 accumulation"), nc.Block() as block:
            P.emit(block, final_waits=fin)
        build_program.stats = (len(P.ops), P.nwaits)
    return nc


_CACHE = {}


def kernel(**inputs):
    consts = _consts()
    if "nc" not in _CACHE:
        _CACHE["nc"] = build_program(NB)
    nc = _CACHE["nc"]
    f32 = lambda a: np.ascontiguousarray(np.asarray(a, dtype=np.float32))
    x = f32(inputs["x"])
    p = f32(inputs["p"])[0]
    shared = {
        "rel_bias": f32(inputs["rel_bias"]),
        "g_attn": f32(inputs["g_attn"]).reshape(1, D),
        "w_in": f32(inputs["w_in"])[0],
        "b_f": f32(inputs["b_f"]).reshape(1, 8),
        "gq_moba": f32(inputs["gq_moba"]).reshape(1, 64),
        "gk_moba": f32(inputs["gk_moba"]).reshape(1, 64),
        "gq_fox": f32(inputs["gq_fox"]).reshape(1, 64),
        "gk_fox": f32(inputs["gk_fox"]).reshape(1, 64),
        "w_out": f32(inputs["w_out"])[0],
        "g_mlp": f32(inputs["g_mlp"]).reshape(1, D),
        "w_up": f32(inputs["w_up"])[0],
        "w_down": f32(inputs["w_down"])[0],
        "g_ple": f32(inputs["g_ple"]).reshape(1, D),
        "w_ple_gate": f32(inputs["w_ple_gate"])[0],
        "w_ple_proj": f32(inputs["w_ple_proj"])[0],
    }
    shared.update(consts)
    in_maps = []
    for c in range(NCORES):
        m = dict(shared)
        m["x"] = np.ascontiguousarray(x[c * NB:(c + 1) * NB])
        m["p"] = np.ascontiguousarray(p[c * NB:(c + 1) * NB])
        in_maps.append(m)
    res = run_bass_kernel_spmd(nc, in_maps, core_ids=list(range(NCORES)))
    out = np.concatenate([np.asarray(r["y"], dtype=np.float32) for r in res.results], axis=0)
    return out
```
